# Optimizing a Trainium2 kernel written in Bass

```python
import math
import jax, jax.numpy as jnp
from jax import lax
import numpy as np

D_MODEL = 1024
BATCH = 8
SEQ = 4096
DEPTH = 4

CHUNK = 64
N_META = 16
Q_BLOCK = 128
HEAD_DIM = 64
SB_HEADS = 8
MLA_HEADS = 8
MLA_NOPE = 64
MLA_ROPE = 32
MLA_V = 64
MLA_Q_RANK = 256
MLA_KV_RANK = 256
ROPE_BASE = 10000.0
SWA_HEADS = 8
SWA_KV_HEADS = 2
WINDOW = 128
WINDOW_CHUNKS = WINDOW // CHUNK
BAND_BACK = WINDOW + CHUNK
FOX_HEADS = 8
FORGET_BIAS_INIT = 3.0

A_WIDTH = SB_HEADS * HEAD_DIM
B_WIDTH = MLA_HEADS * MLA_V
C_WIDTH = SWA_HEADS * HEAD_DIM
C_KV_WIDTH = SWA_KV_HEADS * HEAD_DIM
D_WIDTH = FOX_HEADS * HEAD_DIM
EVEN_WIDTH = A_WIDTH + B_WIDTH
ODD_WIDTH = C_WIDTH + D_WIDTH
EVEN_IN = 3 * A_WIDTH + MLA_Q_RANK + MLA_KV_RANK + MLA_ROPE + EVEN_WIDTH
ODD_IN = C_WIDTH + 2 * C_KV_WIDTH + 3 * D_WIDTH + FOX_HEADS + ODD_WIDTH
N_EVEN = (DEPTH + 1) // 2
N_ODD = DEPTH // 2

DN_ALPHA = (2 * DEPTH) ** 0.25
DN_BETA = (8 * DEPTH) ** -0.25
LN_EPS = 1e-5
RMS_EPS = 1e-6
NEG = -1e30

kernel_name = 'hybrid_chunk_causal_sb_mla_swa_fox'


def _split(t, sizes):
    out, start = [], 0
    for w in sizes:
        out.append(t[..., start:start + w])
        start += w
    return out


def _layer_norm(x, g, b):
    xf = x.astype(jnp.float32)
    mu = jnp.mean(xf, -1, keepdims=True)
    var = jnp.mean(jnp.square(xf - mu), -1, keepdims=True)
    return ((xf - mu) * lax.rsqrt(var + LN_EPS) * g.astype(jnp.float32) + b.astype(jnp.float32)).astype(x.dtype)


def _rms_norm(x, g):
    xf = x.astype(jnp.float32)
    return (xf * lax.rsqrt(jnp.mean(xf * xf, -1, keepdims=True) + RMS_EPS) * g.astype(jnp.float32)).astype(x.dtype)


def _rope(x, pos):
    half = x.shape[-1] // 2
    inv = ROPE_BASE ** (-jnp.arange(half, dtype=jnp.float32) / half)
    ang = pos.astype(jnp.float32)[:, None] * inv[None, :]
    cos = jnp.cos(ang)[None, :, None, :]
    sin = jnp.sin(ang)[None, :, None, :]
    xf = x.astype(jnp.float32)
    x1, x2 = xf[..., :half], xf[..., half:]
    return jnp.concatenate([x1 * cos - x2 * sin, x2 * cos + x1 * sin], -1).astype(x.dtype)


def _chunk_index(pos):
    return jnp.where(pos < N_META, 0, 1 + (pos - N_META) // CHUNK)


def _alibi_slopes(n):
    return jnp.asarray([2.0 ** (-8.0 * (h + 1) / n) for h in range(n)], dtype=jnp.float32)


def _stick_breaking(q, k, v):
    n = q.shape[1]
    scale = HEAD_DIM ** -0.5
    outs = []
    for q0 in range(0, n, Q_BLOCK):
        q1 = q0 + Q_BLOCK
        z = jnp.einsum('bqhd,bkhd->bhqk', q[:, q0:q1], k[:, :q1]).astype(jnp.float32) * scale
        past = jnp.arange(q1)[None, :] < jnp.arange(q0, q1)[:, None]
        log_stay = jnp.where(past, jax.nn.log_sigmoid(-z), 0.0)
        log_after = lax.cumsum(log_stay, axis=3, reverse=True) - log_stay
        w = jnp.where(past, jnp.exp(jax.nn.log_sigmoid(z) + log_after), 0.0)
        outs.append(jnp.einsum('bhqk,bkhd->bqhd', w.astype(v.dtype), v[:, :q1]))
    return jnp.concatenate(outs, axis=1)


def _mla(q_nope, q_rope, k_nope, k_rope, v, chunk):
    n = q_nope.shape[1]
    scale = (MLA_NOPE + MLA_ROPE) ** -0.5
    outs = []
    for q0 in range(0, n, Q_BLOCK):
        q1 = q0 + Q_BLOCK
        k1 = min(n, q1 + CHUNK)
        s = (jnp.einsum('bqhd,bkhd->bhqk', q_nope[:, q0:q1], k_nope[:, :k1])
             + jnp.einsum('bqhr,bkr->bhqk', q_rope[:, q0:q1], k_rope[:, :k1])).astype(jnp.float32) * scale
        vis = chunk[None, :k1] <= chunk[q0:q1, None]
        p = jax.nn.softmax(jnp.where(vis, s, NEG), axis=-1)
        outs.append(jnp.einsum('bhqk,bkhd->bqhd', p.astype(v.dtype), v[:, :k1]))
    return jnp.concatenate(outs, axis=1)


def _swa_sinks(q, k, v, sinks, chunk):
    b, n, hq, d = q.shape
    group = SWA_HEADS // SWA_KV_HEADS
    qg = q.reshape(b, n, SWA_KV_HEADS, group, d)
    slopes = _alibi_slopes(SWA_HEADS).reshape(SWA_KV_HEADS, group)
    sink = sinks.astype(jnp.float32).reshape(SWA_KV_HEADS, group)
    scale = HEAD_DIM ** -0.5
    outs = []
    for q0 in range(0, n, Q_BLOCK):
        q1 = q0 + Q_BLOCK
        k0 = max(N_META, q0 - BAND_BACK)
        k1 = min(n, q1 + CHUNK)
        kidx = jnp.concatenate([jnp.arange(N_META), jnp.arange(k0, k1)])
        kk = jnp.concatenate([k[:, :N_META], k[:, k0:k1]], axis=1)
        vv = jnp.concatenate([v[:, :N_META], v[:, k0:k1]], axis=1)
        tq = jnp.arange(q0, q1)
        dist = jnp.abs(tq[:, None] - kidx[None, :]).astype(jnp.float32)
        s = (jnp.einsum('bqgnd,bkgd->bgnqk', qg[:, q0:q1], kk).astype(jnp.float32) * scale
             - slopes[:, :, None, None] * dist)
        ct = chunk[q0:q1][:, None]
        cs = chunk[kidx][None, :]
        vis = (cs <= ct) & ((ct - cs <= WINDOW_CHUNKS) | (kidx[None, :] < N_META))
        s = jnp.where(vis, s, NEG)
        sink_col = jnp.broadcast_to(sink[None, :, :, None, None], s.shape[:-1] + (1,))
        p = jax.nn.softmax(jnp.concatenate([s, sink_col], axis=-1), axis=-1)[..., :-1]
        o = jnp.einsum('bgnqk,bkgd->bqgnd', p.astype(v.dtype), vv)
        outs.append(o.reshape(b, q1 - q0, hq, d))
    return jnp.concatenate(outs, axis=1)


def _forgetting(q, k, v, log_f):
    n = q.shape[1]
    scale = HEAD_DIM ** -0.5
    cum = jnp.cumsum(log_f, axis=1).transpose(0, 2, 1)
    outs = []
    for q0 in range(0, n, Q_BLOCK):
        q1 = q0 + Q_BLOCK
        s = (jnp.einsum('bqhd,bkhd->bhqk', q[:, q0:q1], k[:, :q1]).astype(jnp.float32) * scale
             + cum[:, :, q0:q1, None] - cum[:, :, None, :q1])
        vis = jnp.arange(q1)[None, :] <= jnp.arange(q0, q1)[:, None]
        p = jax.nn.softmax(jnp.where(vis, s, NEG), axis=-1)
        outs.append(jnp.einsum('bhqk,bkhd->bqhd', p.astype(v.dtype), v[:, :q1]))
    return jnp.concatenate(outs, axis=1)


def _even_mixer(h, w_in, g_cq, g_ckv, w_uq, w_ukv, w_out, chunk):
    b, n, _ = h.shape
    proj = h @ w_in
    qa, ka, va, cq, ckv, kr, gate = _split(
        proj, [A_WIDTH, A_WIDTH, A_WIDTH, MLA_Q_RANK, MLA_KV_RANK, MLA_ROPE, EVEN_WIDTH])
    heads = lambda t, nh: t.reshape(b, n, nh, -1)
    o_a = _stick_breaking(heads(qa, SB_HEADS), heads(ka, SB_HEADS), heads(va, SB_HEADS))
    pos = jnp.arange(n)
    q = (_rms_norm(cq, g_cq) @ w_uq).reshape(b, n, MLA_HEADS, MLA_NOPE + MLA_ROPE)
    kv = (_rms_norm(ckv, g_ckv) @ w_ukv).reshape(b, n, MLA_HEADS, MLA_NOPE + MLA_V)
    q_rope = _rope(q[..., MLA_NOPE:], pos)
    k_rope = _rope(kr[:, :, None, :], pos)[:, :, 0]
    o_b = _mla(q[..., :MLA_NOPE], q_rope, kv[..., :MLA_NOPE], k_rope, kv[..., MLA_NOPE:], chunk)
    mixed = jnp.concatenate([o_a.reshape(b, n, A_WIDTH), o_b.reshape(b, n, B_WIDTH)], axis=-1)
    return (mixed * jax.nn.silu(gate)) @ w_out


def _odd_mixer(h, w_in, b_forget, sinks, w_out, chunk):
    b, n, _ = h.shape
    proj = h @ w_in
    qc, kc, vc, qd, kd, vd, fz, gate = _split(
        proj, [C_WIDTH, C_KV_WIDTH, C_KV_WIDTH, D_WIDTH, D_WIDTH, D_WIDTH, FOX_HEADS, ODD_WIDTH])
    heads = lambda t, nh: t.reshape(b, n, nh, -1)
    o_c = _swa_sinks(heads(qc, SWA_HEADS), heads(kc, SWA_KV_HEADS), heads(vc, SWA_KV_HEADS), sinks, chunk)
    log_f = jax.nn.log_sigmoid(fz.astype(jnp.float32) + b_forget.astype(jnp.float32))
    o_d = _forgetting(heads(qd, FOX_HEADS), heads(kd, FOX_HEADS), heads(vd, FOX_HEADS), log_f)
    mixed = jnp.concatenate([o_c.reshape(b, n, C_WIDTH), o_d.reshape(b, n, D_WIDTH)], axis=-1)
    return (mixed * jax.nn.silu(gate)) @ w_out


def setup_inputs(seed: int = 0) -> dict:
    key = jax.random.key(seed)
    ks = jax.random.split(key, 15)
    nrm = lambda k, shape, scale: jax.random.normal(k, shape, jnp.float32) * scale
    return {
        'x': nrm(ks[0], (BATCH, SEQ, D_MODEL), 1.0),
        'meta_tokens': nrm(ks[1], (N_META, D_MODEL), 1.0),
        'w_in_even': nrm(ks[2], (N_EVEN, D_MODEL, EVEN_IN), D_MODEL ** -0.5),
        'g_cq': 1.0 + nrm(ks[3], (N_EVEN, MLA_Q_RANK), 0.02),
        'g_ckv': 1.0 + nrm(ks[4], (N_EVEN, MLA_KV_RANK), 0.02),
        'w_uq': nrm(ks[5], (N_EVEN, MLA_Q_RANK, MLA_HEADS * (MLA_NOPE + MLA_ROPE)), MLA_Q_RANK ** -0.5),
        'w_ukv': nrm(ks[6], (N_EVEN, MLA_KV_RANK, MLA_HEADS * (MLA_NOPE + MLA_V)), MLA_KV_RANK ** -0.5),
        'w_out_even': nrm(ks[7], (N_EVEN, EVEN_WIDTH, D_MODEL), DN_BETA * EVEN_WIDTH ** -0.5),
        'w_in_odd': nrm(ks[8], (N_ODD, D_MODEL, ODD_IN), D_MODEL ** -0.5),
        'b_forget': FORGET_BIAS_INIT + nrm(ks[9], (N_ODD, FOX_HEADS), 0.1),
        'sink_logits': nrm(ks[10], (N_ODD, SWA_HEADS), 0.5),
        'w_out_odd': nrm(ks[11], (N_ODD, ODD_WIDTH, D_MODEL), DN_BETA * ODD_WIDTH ** -0.5),
        'ln_gain': 1.0 + nrm(ks[12], (DEPTH, D_MODEL), 0.02),
        'ln_bias': nrm(ks[13], (DEPTH, D_MODEL), 0.02),
    }


def reference(x, meta_tokens, w_in_even, g_cq, g_ckv, w_uq, w_ukv, w_out_even,
              w_in_odd, b_forget, sink_logits, w_out_odd, ln_gain, ln_bias):
    b, s, d = x.shape
    n = s + N_META
    n_pad = -(-n // Q_BLOCK) * Q_BLOCK
    h = jnp.concatenate([jnp.broadcast_to(meta_tokens[None].astype(x.dtype), (b, N_META, d)), x,
                         jnp.zeros((b, n_pad - n, d), x.dtype)], axis=1)
    chunk = _chunk_index(jnp.arange(n_pad))
    for layer in range(DEPTH):
        i = layer // 2
        if layer % 2 == 0:
            y = _even_mixer(h, w_in_even[i], g_cq[i], g_ckv[i], w_uq[i], w_ukv[i], w_out_even[i], chunk)
        else:
            y = _odd_mixer(h, w_in_odd[i], b_forget[i], sink_logits[i], w_out_odd[i], chunk)
        h = _layer_norm(DN_ALPHA * h + y, ln_gain[layer], ln_bias[layer])
    return h[:, N_META:N_META + s]
```

```python
import contextlib
import math
import numpy as np
import concourse.bass as bass
import concourse.mybir as mybir
from concourse.bass_utils import run_bass_kernel_spmd

F32 = mybir.dt.float32
BF16 = mybir.dt.bfloat16
AF = mybir.ActivationFunctionType
ALU = mybir.AluOpType

ENGS = ("pe", "act", "dve", "pool", "sp")
NDMA = 8

D = 1024
HD = 64
NEGM = -30000.0
DN_ALPHA = 8.0 ** 0.25
LN_EPS = 1e-5
RMS_EPS = 1e-6
EVEN_IN = 3104
ODD_IN = 3336


class K:
    def __init__(self, nc):
        self.nc = nc
        self.stack = contextlib.ExitStack()
        self.prog = {e: [] for e in ENGS}
        self.cnt = {e: 0 for e in ENGS}
        self.seen = {e: {} for e in ENGS}
        self.lastw = {}
        self.readers = {}
        self.sem = {}
        for e in ENGS:
            self.sem[("E", e)] = self.stack.enter_context(nc.semaphore("s_" + e))
        self.dma_n = {"sp": 0, "pool": 0, "act": 0}
        self.dma_val = {}
        for q in ("sp", "pool", "act"):
            for i in range(NDMA):
                self.sem[("D", q, i)] = self.stack.enter_context(nc.semaphore(f"d_{q}{i}"))
                self.dma_val[(q, i)] = 0

    def sbuf(self, name, shape, dtype):
        return self.stack.enter_context(self.nc.sbuf_tensor(name, list(shape), dtype))

    def psum(self, name, shape, dtype):
        return self.stack.enter_context(self.nc.psum_tensor(name, list(shape), dtype))

    def _deps(self, eng, reads, writes):
        deps = {}

        def add(ev):
            if ev is None:
                return
            k, v = ev
            if k == ("E", "pe") and eng == "pe":
                return
            if deps.get(k, 0) < v:
                deps[k] = v

        for r in reads:
            add(self.lastw.get(r))
        for w in writes:
            add(self.lastw.get(w))
            for ev in self.readers.get(w, ()):
                add(ev)
        waits = []
        seen = self.seen[eng]
        for k, v in deps.items():
            if seen.get(k, 0) < v:
                seen[k] = v
                waits.append((k, v))
        return waits

    def _commit(self, ev, reads, writes):
        for w in writes:
            self.lastw[w] = ev
            self.readers[w] = []
        for r in reads:
            self.readers.setdefault(r, []).append(ev)

    def op(self, eng, fn, reads=(), writes=()):
        waits = self._deps(eng, reads, writes)
        self.cnt[eng] += 1
        ev = (("E", eng), self.cnt[eng])
        self.prog[eng].append((waits, fn, ev[0], 1))
        self._commit(ev, reads, writes)

    def dma(self, q, out, in_, reads=(), writes=(), **kw):
        waits = self._deps(q, reads, writes)
        n = self.dma_n[q]
        self.dma_n[q] += 1
        slot = n % NDMA
        key = ("D", q, slot)
        prev = self.dma_val[(q, slot)]
        if prev and self.seen[q].get(key, 0) < prev:
            self.seen[q][key] = prev
            waits.append((key, prev))
        val = prev + 16
        self.dma_val[(q, slot)] = val
        fn = I("dma_start", out=out, in_=in_, **kw)
        self.prog[q].append((waits, fn, key, 16))
        self._commit((key, val), reads, writes)

    def barrier(self):
        evs = []
        for (q, i), v in self.dma_val.items():
            if v:
                evs.append((("D", q, i), v))
        for e in ENGS:
            if self.cnt[e]:
                evs.append((("E", e), self.cnt[e]))
        for eng in ENGS:
            waits = []
            for k, v in evs:
                if k == ("E", eng):
                    continue
                if self.seen[eng].get(k, 0) < v:
                    self.seen[eng][k] = v
                    waits.append((k, v))
            if waits:
                self.prog[eng].append((waits, None, None, 0))
        self.lastw = {}
        self.readers = {}

    def emit(self):
        nc = self.nc
        sem = self.sem

        def replay(name):
            def run(e):
                for waits, fn, key, inc in self.prog[name]:
                    for k, v in waits:
                        e.wait_ge(sem[k], v)
                    if fn is not None:
                        fn(e).then_inc(sem[key], inc)
            return run

        with nc.Block() as block:
            block.tensor(replay("pe"))
            block.scalar(replay("act"))
            block.vector(replay("dve"))
            block.gpsimd(replay("pool"))
            block.sync(replay("sp"))
        self.stack.close()


def I(name, *args, **kw):
    return lambda e: getattr(e, name)(*args, **kw)


def pipeline(k, items, stages, lags=None):
    n = len(items)
    if lags is None:
        lags = list(range(len(stages)))
    for s in range(n + max(lags)):
        for st, lag in zip(stages, lags):
            i = s - lag
            if 0 <= i < n:
                st(i, items[i])


def _chunk_of(p):
    return 0 if p < 16 else 1 + (p - 16) // 64


def make_consts(NB):
    import ml_dtypes
    bf = ml_dtypes.bfloat16
    NT = NB * 128
    c = {}
    c["ident"] = np.eye(128, dtype=np.float32)
    l = np.arange(128)
    c["maskS"] = np.where(l[None, :] < l[:, None], 0.0, NEGM).astype(bf)
    c["maskF"] = np.where(l[:, None] <= l[None, :], 0.0, NEGM).astype(bf)
    cl = np.array([_chunk_of(128 + i) for i in range(128)])
    c["maskM"] = np.where(cl[:, None] <= cl[None, :], 0.0, NEGM).astype(bf)
    mn = np.full((128, 128), NEGM, np.float32)
    mn[:16, 80:] = 0.0
    c["maskN"] = mn.astype(bf)
    half = 16
    inv = 10000.0 ** (-np.arange(half, dtype=np.float32) / half)
    pos = np.arange(NT, dtype=np.float32)
    ang = pos[:, None] * inv[None, :]
    c["cos"] = np.cos(ang).astype(np.float32).reshape(NB, 128, 16).transpose(1, 0, 2).copy()
    c["sin"] = np.sin(ang).astype(np.float32).reshape(NB, 128, 16).transpose(1, 0, 2).copy()
    slopes = np.array([2.0 ** (-8.0 * (h + 1) / 8) for h in range(8)], np.float32)
    sw = np.zeros((8, 128, 5, 128), np.float32)
    QB = 4
    tq = QB * 128 + l

    def vis_band(s, t):
        cs, ct = _chunk_of(s), _chunk_of(t)
        return (s >= 16) and (cs <= ct) and (ct - cs <= 2)

    for ri, r in enumerate((-2, -1, 0)):
        ks = (QB + r) * 128 + l
        vis = np.array([[vis_band(s, t) for t in tq] for s in ks])
        dist = np.abs(tq[None, :] - ks[:, None]).astype(np.float32)
        for h in range(8):
            sw[h, :, ri, :] = np.where(vis, -8.0 * slopes[h] * dist, NEGM)
    vis = np.array([[(s < 16) or vis_band(s, t) for t in l] for s in l])
    dist = np.abs(l[None, :] - l[:, None]).astype(np.float32)
    for h in range(8):
        sw[h, :, 3, :] = np.where(vis, -8.0 * slopes[h] * dist, NEGM)
    t1 = 128 + l
    vis = np.array([[vis_band(s, t) for t in t1] for s in l])
    dist = np.abs(t1[None, :] - l[:, None]).astype(np.float32)
    for h in range(8):
        sw[h, :, 4, :] = np.where(vis, -8.0 * slopes[h] * dist, NEGM)
    c["swb"] = sw
    sm = np.zeros((8, 16, 2, 128), np.float32)
    kn = (QB + 1) * 128 + np.arange(16)
    vis = np.array([[vis_band(s, t) for t in tq] for s in kn])
    dist = np.abs(tq[None, :] - kn[:, None]).astype(np.float32)
    for h in range(8):
        sm[h, :, 0, :] = np.where(vis, -8.0 * slopes[h] * dist, NEGM)
        sm[h, :, 1, :] = -8.0 * slopes[h] * (l[None, :] - np.arange(16)[:, None]).astype(np.float32)
    c["swsm"] = sm
    mb = np.zeros((16, 8, NB), np.float32)
    for h in range(8):
        for qb in range(NB):
            mb[:, h, qb] = -slopes[h] * 128.0 * qb
    c["metab"] = mb
    return c


CONST_SPECS = [("ident", F32), ("maskS", BF16), ("maskF", BF16), ("maskM", BF16), ("maskN", BF16),
               ("cos", F32), ("sin", F32), ("swb", F32), ("swsm", F32), ("metab", F32)]


def build(NB, S, depth, stop=None, debug=False):
    NT = NB * 128
    NPAD = NT - 16 - S
    assert NPAD >= 0
    NG = (NB + 3) // 4
    nc = bass.Bass("TRN2", target_bir_lowering=False)
    k = K(nc)
    n_even = (depth + 1) // 2
    n_odd = depth // 2

    def din(name, shape, dt=F32):
        return nc.dram_tensor(name, list(shape), dt, kind="ExternalInput").ap()

    def dscr(name, shape, dt):
        return nc.dram_tensor(name, list(shape), dt, kind=("ExternalOutput" if debug else "Internal")).ap()

    x = din("x", [S, D])
    meta = din("meta_tokens", [16, D])
    w_in_even = din("w_in_even", [n_even, D, EVEN_IN])
    g_cq = din("g_cq", [n_even, 256])
    g_ckv = din("g_ckv", [n_even, 256])
    w_uq = din("w_uq", [n_even, 256, 768])
    w_ukv = din("w_ukv", [n_even, 256, 1024])
    w_out_even = din("w_out_even", [n_even, D, D])
    w_in_odd = din("w_in_odd", [max(n_odd, 1), D, ODD_IN])
    b_forget = din("b_forget", [max(n_odd, 1), 8])
    sink_logits = din("sink_logits", [max(n_odd, 1), 8])
    w_out_odd = din("w_out_odd", [max(n_odd, 1), D, D])
    ln_gain = din("ln_gain", [depth, D])
    ln_bias = din("ln_bias", [depth, D])
    cshape = {"ident": [128, 128], "maskS": [128, 128], "maskF": [128, 128], "maskM": [128, 128],
              "maskN": [128, 128], "cos": [128, NB, 16], "sin": [128, NB, 16],
              "swb": [8, 128, 5, 128], "swsm": [8, 16, 2, 128], "metab": [16, 8, NB]}
    cd = {n: din("c_" + n, cshape[n], dt) for n, dt in CONST_SPECS}
    out = nc.dram_tensor("out", [S, D], F32, kind="ExternalOutput").ap()

    hbuf = dscr("hbuf", [NT, D], F32)
    QTd_ = dscr("QTs", [16, 96, NT], BF16)
    KTd_ = dscr("KTs", [16, 96, NT], BF16)
    Vd_ = dscr("Vs", [16, 128, NB, 65], BF16)
    Gd_ = dscr("Gs", [16, 128, NB, 64], F32)
    zscr = dscr("zscr", [128, D], F32)

    hT = k.sbuf("hT", [128, 8, NT], BF16)
    PA_WORDS = 6200 + NT + 2 * NB * 16
    MW = max(NB * 512, PA_WORDS)
    mreg = k.sbuf("mreg", [128, MW], F32)
    mixg = mreg[:, 0:NB * 512].bitcast(BF16).rearrange("p (h b c) -> p h b c", h=8, b=NB)
    ident_f = k.sbuf("ident_f", [128, 128], F32)
    ident_b = k.sbuf("ident_b", [128, 128], BF16)
    mS = k.sbuf("mS", [128, 128], BF16)
    mF = k.sbuf("mF", [128, 128], BF16)
    mM = k.sbuf("mM", [128, 128], BF16)
    mN = k.sbuf("mN", [128, 128], BF16)
    gq_b = k.sbuf("gq_b", [128, 512], F32)
    wslab = k.sbuf("wslab", [128, 2, 8, 544], BF16)
    wuq_sb = k.sbuf("wuq", [128, 2, 768], BF16)
    wukv_sb = k.sbuf("wukv", [128, 2, 1024], BF16)
    WORK = 8700
    work = k.sbuf("work", [128, WORK], F32)
    small = k.sbuf("small", [128, 64], F32)
    ones_f = k.sbuf("ones_f", [128, 512], F32)
    metab_sb = k.sbuf("metab_sb", [16, 8, NB], F32)
    esink = k.sbuf("esink", [128, 8], F32)
    negcum = k.sbuf("negcum", [128, NB, 8], F32)
    crefB = k.sbuf("crefB", [128, 8, NG], F32)
    ps = k.psum("ps", [128, 8, 512], F32)

    class Bump:
        def __init__(self, reg, size):
            self.reg, self.size, self.off = reg, size, 0

        def _take(self, words):
            o = self.off
            self.off += words
            assert self.off <= self.size, (self.off, self.size)
            return o

        def f32(self, shape):
            n = int(np.prod(shape[1:]))
            o = self._take(n)
            ap = self.reg[:shape[0], o:o + n]
            if len(shape) == 3:
                ap = ap.rearrange("p (a b) -> p a b", b=shape[2])
            return ap

        def bf16(self, shape):
            n = int(np.prod(shape[1:]))
            w = (n + 1) // 2
            o = self._take(w)
            ap = self.reg[:shape[0], o:o + w].bitcast(BF16)[:, 0:n]
            if len(shape) == 3:
                ap = ap.rearrange("p (a b) -> p a b", b=shape[2])
            return ap

    hT_flat = hT[:].rearrange("p c t -> p (c t)")
    wflat = wslab[:].rearrange("p a c n -> p (a c n)")

    def psb(bank):
        return ps[:, bank, :]

    def psb_bf(bank):
        return ps[:, bank, :].bitcast(BF16)

    evac_rr = [0]

    def evac(out_ap, in_ap, reads, writes, scale=None, eng=None):
        if eng is None:
            eng = "act" if evac_rr[0] % 2 == 0 else "dve"
            evac_rr[0] += 1
        if eng == "act":
            if scale is None:
                k.op("act", I("copy", out=out_ap, in_=in_ap), reads=reads, writes=writes)
            else:
                k.op("act", I("mul", out=out_ap, in_=in_ap, mul=scale), reads=reads, writes=writes)
        else:
            if scale is None:
                k.op("dve", I("tensor_copy", out=out_ap, in_=in_ap), reads=reads, writes=writes)
            else:
                k.op("dve", I("tensor_scalar_mul", out=out_ap, in0=in_ap, scalar1=scale), reads=reads, writes=writes)

    k.dma("sp", ident_f[:], cd["ident"], writes=["ident_f"])
    k.dma("sp", mS[:], cd["maskS"], writes=["mS"])
    k.dma("sp", mF[:], cd["maskF"], writes=["mF"])
    k.dma("sp", mM[:], cd["maskM"], writes=["mM"])
    k.dma("sp", mN[:], cd["maskN"], writes=["mN"])
    k.dma("sp", metab_sb[:], cd["metab"], writes=["metab"])
    k.op("dve", I("tensor_copy", out=ident_b[:], in_=ident_f[:]), reads=["ident_f"], writes=["ident_b"])
    k.op("dve", I("memset", ones_f[:], 1.0), writes=["ones_f"])
    k.op("pool", I("memset", work[:, 0:D], 0.0), writes=["wz"])
    k.dma("sp", hbuf[0:16, :], meta, writes=["hb_meta"])
    npieces = 8
    step = (S + npieces - 1) // npieces
    for i in range(npieces):
        a, b = i * step, min(S, (i + 1) * step)
        if a < b:
            k.dma("sp", hbuf[16 + a:16 + b, :], x[a:b, :], writes=[("hb_x", i)])
    if NPAD > 0:
        k.dma("sp", hbuf[16 + S:NT, :], work[0:NPAD, 0:D], reads=["wz"], writes=["hb_pad"])
    k.barrier()

    _wb = Bump(work, WORK)
    _hblks = [_wb.f32([128, D]) for _ in range(2)]

    def hblk(i):
        return _hblks[i]

    def make_hT(tb, src, src_key):
        for c in range(8):
            k.op("pe", I("transpose", out=ps[:, 6 + c // 4, (c % 4) * 128:(c % 4 + 1) * 128],
                                                  in_=src[:, c * 128:(c + 1) * 128], identity=ident_f[:]),
                 reads=[src_key, "ident_f"], writes=[("psT", c // 4)])
        o = hT[:, :, tb * 128:(tb + 1) * 128]
        i_ = ps[:, 6:8, :].rearrange("p a (c t) -> p (a c) t", t=128)
        evac(o, i_, reads=[("psT", 0), ("psT", 1)], writes=[("hT", tb)], eng="act")

    for tb in range(NB):
        hb = hblk(tb % 2)
        k.dma("sp", hb, hbuf[tb * 128:(tb + 1) * 128, :], writes=[("hblk", tb % 2)])
        make_hT(tb, hb, ("hblk", tb % 2))
    k.barrier()
    if stop == "P":
        k.emit()
        return nc

    slab_state = {"n": 0}

    def load_slab(w_ap, c0, ncols):
        buf = slab_state["n"] % 2
        slab_state["n"] += 1
        src = w_ap.rearrange("(kc p) c -> p kc c", p=128)
        for kc in range(8):
            k.dma("pool", wslab[:, buf, kc, 0:ncols], src[:, kc, c0:c0 + ncols], writes=[("slab", buf, kc)])
        return buf

    tgroups = []
    t0 = 0
    while t0 < NT:
        w = min(512, NT - t0)
        tgroups.append((t0, w))
        t0 += w

    pa_rr = [0]

    def proj_fm(buf, col0, npairs, dst, heads, scale=None, rows=64):
        for p in range(npairs):
            for (t0, W) in tgroups:
                bank = pa_rr[0] % 4
                pa_rr[0] += 1
                for kc in range(8):
                    k.op("pe", I("matmul",
                        ps[:, bank, 0:W], lhsT=wslab[:, buf, kc, col0 + p * 128:col0 + (p + 1) * 128],
                        rhs=hT[:, kc, t0:t0 + W], start=(kc == 0), stop=(kc == 7)),
                        reads=[("slab", buf, kc)] + [("hT", t0 // 128 + j) for j in range(W // 128)],
                        writes=[("bk", bank)])
                st = stage_fm[bank % 2]
                evac(st[:, 0:W], ps[:, bank, 0:W], reads=[("bk", bank)], writes=[("stfm", bank % 2)], scale=scale)
                for hh in range(2):
                    h = heads[2 * p + hh]
                    if h is None:
                        continue
                    k.dma("sp", dst[h, 0:64, t0:t0 + W], st[hh * 64:(hh + 1) * 64, 0:W],
                          reads=[("stfm", bank % 2)], writes=[("dst", id(dst), h, t0)])

    _ab = Bump(mreg, MW)
    stage_fm = [_ab.bf16([128, 512]) for _ in range(2)]
    stage_v = [_ab.bf16([128, 8, 65]) for _ in range(2)]
    stage_g = [_ab.f32([128, 512]) for _ in range(2)]
    cn_bf = _ab.bf16([128, 512])
    cT_sb = _ab.bf16([128, 4, 128])
    qtm = _ab.bf16([128, 8, 96])
    ktm = _ab.bf16([128, 8, 96])
    qkT_sb = [_ab.bf16([128, 8, 128]) for _ in range(2)]
    rtmp = _ab.f32([128, 4, 128])
    krope = _ab.bf16([128, 32])
    sq_junk = _ab.f32([128, 256])
    fz_sb = _ab.f32([128, NT])
    fz_e = _ab.f32([128, 512])
    cosT = _ab.f32([128, NB, 16])
    sinT = _ab.f32([128, NB, 16])

    def proj_tm_block(buf, tb, bank, col0, ncols, nbanks=1):
        for kc in range(8):
            for bb in range(nbanks):
                c_lo = col0 + bb * 512
                c_n = min(512, ncols - bb * 512)
                k.op("pe", I("matmul",
                    ps[:, bank + bb, 0:c_n], lhsT=hT[:, kc, tb * 128:(tb + 1) * 128],
                    rhs=wslab[:, buf, kc, c_lo:c_lo + c_n], start=(kc == 0), stop=(kc == 7)),
                    reads=[("slab", buf, kc), ("hT", tb)], writes=[("bk", bank + bb)])

    def proj_v(buf, col0, nh, hbase, ones_col=True):
        for tb in range(NB):
            bank = pa_rr[0] % 4
            pa_rr[0] += 1
            proj_tm_block(buf, tb, bank, col0, nh * 64)
            st = stage_v[bank % 2]
            evac(st[:, 0:nh, 0:64], ps[:, bank, 0:nh * 64].rearrange("p (h c) -> p h c", c=64),
                 reads=[("bk", bank)], writes=[("stv", bank % 2)])
            k.dma("sp", Vd_[hbase:hbase + nh, :, tb, :].rearrange("h p c -> p h c"), st[:, 0:nh, :],
                  reads=[("stv", bank % 2)], writes=[("Vd", hbase, tb)])

    def proj_gate(buf, col0, hbase):
        for tb in range(NB):
            bank = pa_rr[0] % 4
            pa_rr[0] += 1
            proj_tm_block(buf, tb, bank, col0, 512)
            st = stage_g[bank % 2]
            k.op("act", I("activation", out=st, in_=ps[:, bank, :], func=AF.Silu),
                 reads=[("bk", bank)], writes=[("stg", bank % 2)])
            k.dma("sp", Gd_[hbase:hbase + 8, :, tb, :].rearrange("h p c -> p h c"),
                  st.rearrange("p (h c) -> p h c", c=64), reads=[("stg", bank % 2)], writes=[("Gd", hbase, tb)])

    def rope_ops(x1, x2, o1, o2, cs, sn, shape_key):
        t = [rtmp[:, i, 0:int(np.prod(x1.shape[1:]))] for i in range(4)]
        if len(x1.shape) == 3:
            t = [ti.rearrange("p (a b) -> p a b", b=x1.shape[2]) for ti in t]
        rk = ["rt0", "rt1", "rt2", "rt3"]
        k.op("dve", I("tensor_tensor", out=t[0], in0=x1, in1=cs, op=ALU.mult), reads=shape_key, writes=[rk[0]])
        k.op("dve", I("tensor_tensor", out=t[1], in0=x2, in1=sn, op=ALU.mult), reads=shape_key, writes=[rk[1]])
        k.op("dve", I("tensor_tensor", out=t[2], in0=x2, in1=cs, op=ALU.mult), reads=shape_key, writes=[rk[2]])
        k.op("dve", I("tensor_tensor", out=t[3], in0=x1, in1=sn, op=ALU.mult), reads=shape_key, writes=[rk[3]])
        return t, rk

    for layer in range(depth):
        li = layer // 2
        even = (layer % 2 == 0)
        last = (layer == depth - 1)
        w_in = (w_in_even if even else w_in_odd)[li]
        w_out = (w_out_even if even else w_out_odd)[li]

        k.op("pool", I("memset", stage_v[0][:, :, 64:65], 1.0), writes=[("stv", 0)])
        k.op("pool", I("memset", stage_v[1][:, :, 64:65], 1.0), writes=[("stv", 1)])

        if even:
            k.dma("sp", cosT, cd["cos"], writes=["cosT"])
            k.dma("sp", sinT, cd["sin"], writes=["sinT"])
            k.dma("sp", gq_b[:, 0:256], g_cq[li:li + 1, :].partition_broadcast(128), writes=["gq_b"])
            k.dma("sp", gq_b[:, 256:512], g_ckv[li:li + 1, :].partition_broadcast(128), writes=["gq_b2"])
            for kc in range(2):
                k.dma("pool", wuq_sb[:, kc, :], w_uq[li, kc * 128:(kc + 1) * 128, :], writes=[("wuq", kc)])
                k.dma("pool", wukv_sb[:, kc, :], w_ukv[li, kc * 128:(kc + 1) * 128, :], writes=[("wukv", kc)])
            b0 = load_slab(w_in, 0, 512)
            b1 = load_slab(w_in, 512, 512)
            proj_fm(b0, 0, 4, QTd_, list(range(8)), scale=0.125)
            b2 = load_slab(w_in, 1024, 512)
            proj_fm(b1, 0, 4, KTd_, list(range(8)))
            if stop == ("a", 1):
                k.barrier(); k.emit(); return nc
            b3 = load_slab(w_in, 1536, 544)
            proj_v(b2, 0, 8, 0)
            if stop == ("a", 2):
                k.barrier(); k.emit(); return nc
            b4 = load_slab(w_in, 2080, 512)
            for tb in range(NB):
                for kc in range(8):
                    k.op("pe", I("matmul", ps[:, 0, :], lhsT=hT[:, kc, tb * 128:(tb + 1) * 128],
                                                         rhs=wslab[:, b3, kc, 0:512], start=(kc == 0), stop=(kc == 7)),
                         reads=[("slab", b3, kc), ("hT", tb)], writes=[("bk", 0)])
                for kc in range(8):
                    k.op("pe", I("matmul", ps[:, 1, 0:32], lhsT=hT[:, kc, tb * 128:(tb + 1) * 128],
                                                         rhs=wslab[:, b3, kc, 512:544], start=(kc == 0), stop=(kc == 7)),
                         reads=[("slab", b3, kc), ("hT", tb)], writes=[("bk", 1)])
                for j in range(2):
                    k.op("act", I("activation", out=sq_junk, in_=ps[:, 0, j * 256:(j + 1) * 256], func=AF.Square,
                                                            accum_out=small[:, j:j + 1]),
                         reads=[("bk", 0)], writes=["sqj", ("ssq", j)])
                k.op("act", I("activation", out=small[:, 2:4], in_=small[:, 0:2], func=AF.Ln, scale=1.0 / 256, bias=RMS_EPS),
                     reads=[("ssq", 0), ("ssq", 1)], writes=["lnms"])
                k.op("act", I("activation", out=small[:, 4:6], in_=small[:, 2:4], func=AF.Exp, scale=-0.5),
                     reads=["lnms"], writes=["rstd"])
                if stop == ("m", 1):
                    k.barrier(); k.emit(); return nc
                for j in range(2):
                    k.op("dve", I("scalar_tensor_tensor",
                        out=cn_bf[:, j * 256:(j + 1) * 256], in0=ps[:, 0, j * 256:(j + 1) * 256], scalar=small[:, 4 + j:5 + j],
                        in1=gq_b[:, j * 256:(j + 1) * 256], op0=ALU.mult, op1=ALU.mult),
                        reads=[("bk", 0), "rstd", "gq_b", "gq_b2"], writes=[("cn", j)])
                pbt = psb_bf(2)
                for j in range(4):
                    k.op("pe", I("transpose", out=pbt[:, j * 128:(j + 1) * 128], in_=cn_bf[:, j * 128:(j + 1) * 128],
                                                          identity=ident_b[:]),
                         reads=[("cn", j // 2), "ident_b"], writes=[("bk", 2)])
                evac(cT_sb, pbt[:, 0:512].rearrange("p (a b) -> p a b", b=128), reads=[("bk", 2)], writes=["cT"], eng="dve")
                if stop == ("m", 2):
                    k.barrier(); k.emit(); return nc
                for (bank, c0, cn_) in ((3, 0, 512), (4, 512, 256)):
                    for kc in range(2):
                        k.op("pe", I("matmul",
                            ps[:, bank, 0:cn_], lhsT=cT_sb[:, kc, :], rhs=wuq_sb[:, kc, c0:c0 + cn_], start=(kc == 0), stop=(kc == 1)),
                            reads=["cT", ("wuq", kc)], writes=[("bk", bank)])
                for (bank, c0) in ((5, 0), (6, 512)):
                    for kc in range(2):
                        k.op("pe", I("matmul",
                            ps[:, bank, :], lhsT=cT_sb[:, 2 + kc, :], rhs=wukv_sb[:, kc, c0:c0 + 512], start=(kc == 0), stop=(kc == 1)),
                            reads=["cT", ("wukv", kc)], writes=[("bk", bank)])
                if stop == ("m", 3):
                    k.barrier(); k.emit(); return nc
                qps = ps[:, 3:5, :].rearrange("p a b -> p (a b)")[:, 0:768].rearrange("p (h c) -> p h c", c=96)
                kvps = ps[:, 5:7, :].rearrange("p a b -> p (a b)").rearrange("p (h c) -> p h c", c=128)
                cs8 = cosT[:, tb, :].unsqueeze(1).broadcast_to([128, 8, 16])
                sn8 = sinT[:, tb, :].unsqueeze(1).broadcast_to([128, 8, 16])
                import os
                SK = os.environ.get("SK", "")
                if "a" not in SK:
                    t, rk = rope_ops(qps[:, :, 64:80], qps[:, :, 80:96], None, None, cs8, sn8, [("bk", 3), ("bk", 4), "cosT", "sinT"])
                if "b" not in SK:
                    k.op("dve", I("tensor_tensor", out=t[0], in0=t[0], in1=t[1], op=ALU.subtract), reads=rk[0:2], writes=[rk[0]])
                    k.op("dve", I("tensor_tensor", out=t[2], in0=t[2], in1=t[3], op=ALU.add), reads=rk[2:4], writes=[rk[2]])
                    k.op("act", I("copy", out=qtm[:, :, 64:80], in_=t[0]), reads=[rk[0]], writes=[("qtm", 1)])
                    k.op("act", I("copy", out=qtm[:, :, 80:96], in_=t[2]), reads=[rk[2]], writes=[("qtm", 2)])
                if "c" not in SK:
                    k.op("act", I("copy", out=qtm[:, :, 0:64], in_=qps[:, :, 0:64]), reads=[("bk", 3), ("bk", 4)], writes=[("qtm", 0)])
                if stop == ("m", 4):
                    k.barrier(); k.emit(); return nc
                krp = ps[:, 1, 0:32]
                t2, rk2 = rope_ops(krp[:, 0:16], krp[:, 16:32], None, None, cosT[:, tb, :], sinT[:, tb, :], [("bk", 1), "cosT", "sinT"])
                k.op("dve", I("tensor_tensor", out=t2[0], in0=t2[0], in1=t2[1], op=ALU.subtract), reads=rk2[0:2], writes=[rk2[0]])
                k.op("dve", I("tensor_tensor", out=t2[2], in0=t2[2], in1=t2[3], op=ALU.add), reads=rk2[2:4], writes=[rk2[2]])
                k.op("act", I("copy", out=ktm[:, :, 64:80], in_=t2[0].unsqueeze(1).broadcast_to([128, 8, 16])), reads=[rk2[0]], writes=[("ktm", 1)])
                k.op("act", I("copy", out=ktm[:, :, 80:96], in_=t2[2].unsqueeze(1).broadcast_to([128, 8, 16])), reads=[rk2[2]], writes=[("ktm", 2)])
                k.op("act", I("copy", out=ktm[:, :, 0:64], in_=kvps[:, :, 0:64]), reads=[("bk", 5), ("bk", 6)], writes=[("ktm", 0)])
                sv = stage_v[tb % 2]
                k.op("act", I("copy", out=sv[:, :, 0:64], in_=kvps[:, :, 64:128]), reads=[("bk", 5), ("bk", 6)], writes=[("stv", tb % 2)])
                k.dma("sp", Vd_[8:16, :, tb, :].rearrange("h p c -> p h c"), sv, reads=[("stv", tb % 2)], writes=[("Vd", 8, tb)])
                if stop == ("m", 5):
                    k.barrier(); k.emit(); return nc
                for which, src_t, dst_d in ((0, qtm, QTd_), (1, ktm, KTd_)):
                    pb7 = psb_bf(7)
                    for h in range(8):
                        k.op("pe", I("transpose", out=pb7[0:96, h * 128:(h + 1) * 128], in_=src_t[:, h, :],
                                                                                  identity=ident_b[:]),
                             reads=[((("qtm" if which == 0 else "ktm")), j) for j in range(3)] + ["ident_b"], writes=[("bk", 7)])
                    so = qkT_sb[which]
                    evac(so[0:96], pb7[0:96, :].rearrange("p (h t) -> p h t", t=128), reads=[("bk", 7)], writes=[("qkT", which)])
                    k.dma("sp", dst_d[8:16, :, tb * 128:(tb + 1) * 128].rearrange("h p t -> p h t"), so[0:96],
                          reads=[("qkT", which)], writes=[("dstm", which, tb)])
            if stop == ("a", 3):
                k.barrier(); k.emit(); return nc
            b5 = load_slab(w_in, 2592, 512)
            proj_gate(b4, 0, 0)
            proj_gate(b5, 0, 8)
        else:
            k.dma("sp", esink[:], sink_logits[li:li + 1, :].partition_broadcast(128), writes=["esink"])
            k.op("act", I("activation", out=esink[:], in_=esink[:], func=AF.Exp), reads=["esink"], writes=["esink"])
            k.dma("sp", small[0:8, 8:9], b_forget[li:li + 1, :].rearrange("a h -> h a"), writes=["bf"])
            k.op("dve", I("tensor_scalar_mul", out=small[0:8, 9:10], in0=small[0:8, 8:9], scalar1=-1.0), reads=["bf"], writes=["nbf"])
            b0 = load_slab(w_in, 0, 512)
            b1 = load_slab(w_in, 512, 256)
            proj_fm(b0, 0, 4, QTd_, list(range(8)))
            b2 = load_slab(w_in, 768, 512)
            for rep in range(4):
                proj_fm(b1, 0, 1, KTd_, [rep, 4 + rep])
            for tb in range(NB):
                bank = pa_rr[0] % 4
                pa_rr[0] += 1
                proj_tm_block(b1, tb, bank, 128, 128)
                st = stage_v[bank % 2]
                for rep in range(4):
                    evac(st[:, rep:rep + 5:4, 0:64], ps[:, bank, 0:128].rearrange("p (h c) -> p h c", c=64),
                         reads=[("bk", bank)], writes=[("stv", bank % 2)])
                k.dma("sp", Vd_[0:8, :, tb, :].rearrange("h p c -> p h c"), st[:, 0:8, :],
                      reads=[("stv", bank % 2)], writes=[("Vd", 0, tb)])
            b3 = load_slab(w_in, 1280, 512)
            proj_fm(b2, 0, 4, QTd_, [8 + i for i in range(8)])
            b4 = load_slab(w_in, 1792, 520)
            proj_fm(b3, 0, 4, KTd_, [8 + i for i in range(8)])
            b5 = load_slab(w_in, 2312, 512)
            proj_v(b4, 0, 8, 8)
            for (t0, W) in tgroups:
                bank = pa_rr[0] % 4
                pa_rr[0] += 1
                for kc in range(8):
                    k.op("pe", I("matmul",
                        ps[0:8, bank, 0:W], lhsT=wslab[:, b4, kc, 512:520], rhs=hT[:, kc, t0:t0 + W], start=(kc == 0), stop=(kc == 7)),
                        reads=[("slab", b4, kc)] + [("hT", t0 // 128 + j) for j in range(W // 128)], writes=[("bk", bank)])
                k.op("act", I("activation", out=fz_e[0:8, 0:W], in_=ps[0:8, bank, 0:W], func=AF.Exp, scale=-1.0,
                                                                   bias=small[0:8, 9:10]),
                     reads=[("bk", bank), "nbf"], writes=["fz_e"])
                k.op("act", I("activation", out=fz_sb[0:8, t0:t0 + W], in_=fz_e[0:8, 0:W], func=AF.Ln, bias=1.0),
                     reads=["fz_e"], writes=[("fz", t0)])
            prev = None
            for (t0, W) in tgroups:
                init = 0.0 if prev is None else fz_sb[0:8, t0 - 1:t0]
                k.op("dve", I("tensor_tensor_scan",
                    out=fz_sb[0:8, t0:t0 + W], data0=ones_f[0:8, 0:W], data1=fz_sb[0:8, t0:t0 + W], initial=init,
                    op0=ALU.mult, op1=ALU.add), reads=[("fz", t0), "ones_f"] + ([("fz", prev)] if prev is not None else []),
                    writes=[("fz", t0)])
                prev = t0
            allfz = [("fz", t0) for (t0, W) in tgroups]
            for tb in range(NB):
                k.op("pe", I("transpose", out=ps[:, 2, tb * 8:(tb + 1) * 8], in_=fz_sb[0:8, tb * 128:(tb + 1) * 128],
                                                        identity=ident_f[0:8, 0:8]),
                     reads=allfz + ["ident_f"], writes=[("bk", 2)])
            k.op("dve", I("tensor_copy", out=negcum[:].rearrange("p b h -> p (b h)"), in_=ps[:, 2, 0:NB * 8]),
                 reads=[("bk", 2)], writes=["negcum"])
            dg = fz_e[0:8, 0:8 * NG].rearrange("p (h g) -> p h g", g=NG)
            k.op("dve", I("memset", fz_e[0:8, 0:8 * NG], 0.0), reads=["fz_e"], writes=["fz_e"])
            for g in range(NG):
                lastq = min(NB, 4 * g + 4) * 128 - 1
                k.op("dve", I("tensor_tensor",
                    out=dg[:, :, g], in0=ident_f[0:8, 0:8], in1=fz_sb[0:8, lastq:lastq + 1].broadcast_to([8, 8]), op=ALU.mult),
                    reads=allfz + ["ident_f", "fz_e"], writes=["fz_e"])
            k.op("pe", I("matmul", ps[:, 3, 0:8 * NG], lhsT=ones_f[0:8, 0:128], rhs=fz_e[0:8, 0:8 * NG], start=True, stop=True),
                 reads=["fz_e", "ones_f"], writes=[("bk", 3)])
            k.op("dve", I("tensor_scalar_mul", out=crefB[:].rearrange("p h g -> p (h g)"), in0=ps[:, 3, 0:8 * NG], scalar1=-1.0),
                 reads=[("bk", 3)], writes=["crefB"])
            proj_gate(b5, 0, 0)
            b6 = load_slab(w_in, 2824, 512)
            proj_gate(b6, 0, 8)
        k.barrier()
        if stop == ("A", layer):
            k.emit()
            return nc

        wout_sb = wflat[:, 0:8 * 1024].rearrange("p (c n) -> p c n", n=1024)
        for kc in range(8):
            k.dma("pool", wout_sb[:, kc, :], w_out[kc * 128:(kc + 1) * 128, :], writes=[("wout", kc)])

        HB = NT + NT + NB * 65 + (NB % 2) + 2 * NB * 64
        assert 2 * HB <= 8 * NT

        def headbuf(s):
            o = s * HB
            qt = hT_flat[:, o:o + NT]
            kt = hT_flat[:, o + NT:o + 2 * NT]
            v = hT_flat[:, o + 2 * NT:o + 2 * NT + NB * 65].rearrange("p (b c) -> p b c", c=65)
            g = hT_flat[:, o + 2 * NT + NB * 65 + (NB % 2):o + 2 * NT + NB * 65 + (NB % 2) + 2 * NB * 64].bitcast(F32).rearrange("p (b c) -> p b c", c=64)
            return qt, kt, v, g

        hb_sets = [headbuf(0), headbuf(1)]
        _bb = Bump(work, WORK)
        e_f = [_bb.f32([128, 512]) for _ in range(2)]
        sp_f = [_bb.f32([128, 512]) for _ in range(2)]
        P_f = [_bb.f32([128, 514]) for _ in range(2)]
        arg_f = [_bb.f32([128, 512]) for _ in range(2)]
        wT_b = [_bb.bf16([128, 512]) for _ in range(2)]
        E_b = [_bb.bf16([128, 512]) for _ in range(2)]
        sadd = [_bb.f32([128, 3, 128]) for _ in range(2)]
        ssm2 = [_bb.f32([128, 256]) for _ in range(2)]
        Esm = [_bb.bf16([128, 128]) for _ in range(4)]
        swb_sbs = [_bb.f32([128, 5, 128]) for _ in range(2)]
        swsm_sbs = [_bb.f32([128, 2, 128]) for _ in range(2)]
        brow = [_bb.f32([128, 64]) for _ in range(2)]
        Cc = [small[:, 16 + i:17 + i] for i in range(4)]
        rden = [small[:, 24 + i:25 + i] for i in range(8)]
        k.op("dve", I("memset", P_f[0][:, 0:1], 0.0), writes=[("P", 0)])
        k.op("dve", I("memset", P_f[1][:, 0:1], 0.0), writes=[("P", 1)])

        for s_ in range(2):
            qt_, kt_, _, _ = hb_sets[s_]
            k.op("pool", I("memset", qt_[64:128, :], 0.0), writes=[("QT", s_)])
            k.op("pool", I("memset", kt_[64:128, :], 0.0), writes=[("KT", s_)])

        def load_head(h, s):
            qt, kt, v, g = hb_sets[s]
            rows = 96 if (even and h >= 8) else 64
            k.dma("sp", qt[0:rows, :], QTd_[h, 0:rows, :], writes=[("QT", s)])
            k.dma("sp", kt[0:rows, :], KTd_[h, 0:rows, :], writes=[("KT", s)])
            k.dma("sp", v, Vd_[h], writes=[("V", s)])
            k.dma("sp", g, Gd_[h], writes=[("G", s)])
            if (not even) and h < 8:
                k.dma("sp", swb_sbs[s], cd["swb"][h], writes=[("swb", s)])
                k.dma("sp", swsm_sbs[s][0:16], cd["swsm"][h], writes=[("swsm", s)])

        cnt = {"sb": 0, "rd": 0, "acc": 0}

        def sb_head(h, s):
            qt, kt, v, g = hb_sets[s]
            items = []
            for qb in range(NB):
                hi = (qb + 1) * 128
                nfull = hi // 512
                chunks = []
                if hi % 512:
                    chunks.append((nfull * 512, hi - nfull * 512))
                for c in range(nfull - 1, -1, -1):
                    chunks.append((c * 512, 512))
                for ci, (k0, W) in enumerate(chunks):
                    items.append(dict(qb=qb, k0=k0, W=W, first=(ci == 0), lastc=(ci == len(chunks) - 1), idx=len(items)))

            def zb(i):
                return i % 4

            def ab(i):
                return 4 + i % 2

            def accb(qb):
                return 6 + qb % 2

            def stA(i, it):
                qb, k0, W = it["qb"], it["k0"], it["W"]
                b = zb(i)
                k.op("pe", I("matmul", ps[:, b, 0:W], lhsT=qt[0:128, qb * 128:(qb + 1) * 128], rhs=kt[0:128, k0:k0 + W],
                             start=True, stop=not it["first"]),
                     reads=[("QT", s), ("KT", s)], writes=[("z", b)])
                if it["first"]:
                    k.op("pe", I("matmul", ps[:, b, W - 128:W], lhsT=ident_b[:], rhs=mS[:], start=False, stop=True),
                         reads=["ident_b", "mS"], writes=[("z", b)])

            def stBexp(i, it):
                W = it["W"]
                b = zb(i)
                u = i % 2
                k.op("act", I("activation", out=e_f[u][:, 0:W], in_=ps[:, b, 0:W], func=AF.Exp), reads=[("z", b)], writes=[("e", u)])

            def stBln(i, it):
                W = it["W"]
                u = i % 2
                k.op("act", I("activation", out=sp_f[u][:, 0:W], in_=e_f[u][:, 0:W], func=AF.Ln, bias=1.0), reads=[("e", u)], writes=[("sp", u)])

            def stScan(i, it):
                W = it["W"]
                u = i % 2
                pu = (i - 1) % 2
                if it["first"]:
                    init, rd = 0.0, []
                else:
                    init, rd = P_f[pu][:, 0:1], [("P", pu)]
                k.op("dve", I("tensor_tensor_scan", out=P_f[u][:, 0:W][:, ::-1], data0=ones_f[:, 0:W],
                              data1=sp_f[u][:, 0:W][:, ::-1], initial=init, op0=ALU.mult, op1=ALU.add),
                     reads=[("sp", u), "ones_f"] + rd, writes=[("P", u)])

            def stSub(i, it):
                W = it["W"]
                b = zb(i)
                u = i % 2
                k.op("dve", I("tensor_tensor", out=arg_f[u][:, 0:W], in0=ps[:, b, 0:W], in1=P_f[u][:, 0:W], op=ALU.subtract),
                     reads=[("z", b), ("P", u)], writes=[("arg", u)])
                a = ab(i)
                for j in range(W // 128):
                    k.op("pe", I("transpose", out=ps[:, a, j * 128:(j + 1) * 128], in_=arg_f[u][:, j * 128:(j + 1) * 128],
                                 identity=ident_f[:]), reads=[("arg", u), "ident_f"], writes=[("aT", a)])

            def stC(i, it):
                qb, k0, W = it["qb"], it["k0"], it["W"]
                a = ab(i)
                u = i % 2
                k.op("act", I("activation", out=wT_b[u][:, 0:W], in_=ps[:, a, 0:W], func=AF.Exp), reads=[("aT", a)], writes=[("wT", u)])
                acc = accb(qb)
                nb = W // 128
                for j in range(nb):
                    kb = k0 // 128 + j
                    k.op("pe", I("matmul", ps[:, acc, 0:64], lhsT=wT_b[u][:, j * 128:(j + 1) * 128], rhs=v[:, kb, 0:64],
                                 start=(it["first"] and j == 0), stop=(it["lastc"] and j == nb - 1)),
                         reads=[("wT", u), ("V", s)], writes=[("acc", acc)])

            def stD(i, it):
                qb = it["qb"]
                acc = accb(qb)
                if it["lastc"]:
                    k.op("dve", I("tensor_tensor", out=mixg[:, h // 2, qb, (h % 2) * 64:(h % 2) * 64 + 64], in0=ps[:, acc, 0:64], in1=g[:, qb, :], op=ALU.mult),
                         reads=[("acc", acc), ("G", s)], writes=[("mixg", h, qb)])

            pipeline(k, items, [stA, stBexp, stC, stBln, stScan, stSub, stD], [0, 1, 4, 1, 2, 3, 5])

        def kq_head(h, s, kind):
            qt, kt, v, g = hb_sets[s]
            KD = 96 if kind == "mla" else 128
            scale = (96.0 ** -0.5) if kind == "mla" else 0.125
            hl = h - 8
            items = []
            for gi in range(NG):
                q0b = 4 * gi
                nq = min(4, NB - q0b)
                kmax = min(q0b + nq, NB - 1) if kind == "mla" else q0b + nq - 1
                for kb in range(0, kmax + 1):
                    jlo = max(0, kb - q0b - (1 if kind == "mla" else 0))
                    items.append(dict(gi=gi, q0b=q0b, nq=nq, kb=kb, jlo=jlo, firstk=(kb == 0), lastk=(kb == kmax), kmax=kmax))

            def sbk(i):
                return i % 4

            def accb(gi):
                return 4 + gi % 2

            def stA(i, it):
                q0b, nq, kb, jlo = it["q0b"], it["nq"], it["kb"], it["jlo"]
                b = sbk(i)
                Wc = (nq - jlo) * 128
                c0 = (q0b + jlo) * 128
                masks = []
                for j in range(jlo, nq):
                    rel = (q0b + j) - kb
                    if rel == -1:
                        masks.append((j, mN, "mN"))
                    elif rel == 0:
                        masks.append((j, mM if kind == "mla" else mF, "mM" if kind == "mla" else "mF"))
                k.op("pe", I("matmul", ps[:, b, 0:Wc], lhsT=kt[0:KD, kb * 128:(kb + 1) * 128], rhs=qt[0:KD, c0:c0 + Wc],
                                              start=True, stop=(len(masks) == 0)),
                     reads=[("QT", s), ("KT", s)], writes=[("S", b)])
                for mi, (j, mt, mk) in enumerate(masks):
                    k.op("pe", I("matmul", ps[:, b, (j - jlo) * 128:(j - jlo + 1) * 128], lhsT=ident_b[:], rhs=mt[:],
                                                                    start=False, stop=(mi == len(masks) - 1)),
                         reads=["ident_b", mk], writes=[("S", b)])
                if kind == "fox" and it["firstk"]:
                    bi = it["gi"] % 2
                    k.op("dve", I("tensor_scalar", out=brow[bi][:, 0:NB], in0=negcum[:, :, hl], scalar1=crefB[:, hl, it["gi"]:it["gi"] + 1],
                                                          scalar2=None, op0=ALU.add),
                         reads=["negcum", "crefB"], writes=[("brow", bi)])

            def stB(i, it):
                q0b, nq, kb, jlo = it["q0b"], it["nq"], it["kb"], it["jlo"]
                b = sbk(i)
                u = i % 2
                Wc = (nq - jlo) * 128
                if kind == "fox":
                    bi = it["gi"] % 2
                    k.op("act", I("activation", out=E_b[u][:, 0:Wc], in_=ps[:, b, 0:Wc], func=AF.Exp, scale=scale, bias=brow[bi][:, kb:kb + 1]),
                         reads=[("S", b), ("brow", bi)], writes=[("E", u)])
                else:
                    k.op("act", I("activation", out=E_b[u][:, 0:Wc], in_=ps[:, b, 0:Wc], func=AF.Exp, scale=scale),
                         reads=[("S", b)], writes=[("E", u)])

            def stC(i, it):
                q0b, nq, kb, jlo, gi = it["q0b"], it["nq"], it["kb"], it["jlo"], it["gi"]
                u = i % 2
                acc = accb(gi)
                Wc = (nq - jlo) * 128
                k.op("pe", I("matmul", ps[0:65, acc, jlo * 128:nq * 128], lhsT=v[:, kb, :], rhs=E_b[u][:, 0:Wc],
                             start=it["firstk"], stop=it["lastk"], skip_group_check=True),
                     reads=[("E", u), ("V", s)], writes=[("acc", acc)])

            def stD(i, it):
                q0b, nq, gi = it["q0b"], it["nq"], it["gi"]
                if not it["lastk"]:
                    return
                acc = accb(gi)
                fb = 6 + gi % 2
                aS = e_f[gi % 2]
                W = nq * 128
                k.op("dve", I("tensor_copy", out=aS[0:65, 0:W], in_=ps[0:65, acc, 0:W]), reads=[("acc", acc)], writes=[("e", gi % 2)])
                for j in range(nq):
                    k.op("pe", I("transpose", out=ps[:, fb, j * 65:(j + 1) * 65], in_=aS[0:65, j * 128:(j + 1) * 128], identity=ident_f[0:65, 0:65]),
                         reads=[("e", gi % 2), "ident_f"], writes=[("fin", fb)])
                for j in range(nq):
                    qb = q0b + j
                    ri = cnt["rd"] % 8
                    cnt["rd"] += 1
                    k.op("dve", I("reciprocal", out=rden[ri], in_=ps[:, fb, j * 65 + 64:j * 65 + 65]),
                         reads=[("fin", fb)], writes=[("rden", ri)])
                    k.op("dve", I("scalar_tensor_tensor", out=mixg[:, h // 2, qb, (h % 2) * 64:(h % 2) * 64 + 64], in0=ps[:, fb, j * 65:j * 65 + 64],
                                  scalar=rden[ri], in1=g[:, qb, :], op0=ALU.mult, op1=ALU.mult),
                         reads=[("fin", fb), ("rden", ri), ("G", s)], writes=[("mixg", h, qb)])

            pipeline(k, items, [stA, stB, stC, stD], [0, 2, 3, 4])

        def swa_head(h, s):
            qt, kt, v, g = hb_sets[s]
            swb_sb, swsm_sb = swb_sbs[s], swsm_sbs[s]
            items = [dict(qb=qb) for qb in range(NB)]

            def stA(i, it):
                qb = it["qb"]
                b = i % 2
                qs = qt[0:128, qb * 128:(qb + 1) * 128]
                for ri, r in enumerate((-2, -1, 0)):
                    kb = qb + r
                    if kb < 0:
                        continue
                    k.op("pe", I("matmul", ps[:, b, ri * 128:(ri + 1) * 128], lhsT=kt[0:128, kb * 128:(kb + 1) * 128], rhs=qs,
                                                                start=True, stop=True, skip_group_check=True),
                         reads=[("QT", s), ("KT", s)], writes=[("S", b)])
                b2 = 2 + i % 2
                if qb + 1 < NB:
                    k.op("pe", I("matmul", ps[0:16, b2, 0:128], lhsT=kt[0:128, (qb + 1) * 128:(qb + 1) * 128 + 16], rhs=qs,
                                                  start=True, stop=True, skip_group_check=True),
                         reads=[("QT", s), ("KT", s)], writes=[("Sn", b2)])
                if qb >= 1:
                    k.op("pe", I("matmul", ps[0:16, b2, 128:256], lhsT=kt[0:128, 0:16], rhs=qs, start=True, stop=True, skip_group_check=True),
                         reads=[("QT", s), ("KT", s)], writes=[("Sm", b2)])

            def stB(i, it):
                qb = it["qb"]
                b = i % 2
                u = i % 2
                b2 = 2 + i % 2
                if qb >= 2:
                    k.op("dve", I("tensor_tensor", out=sadd[u][:].rearrange("p a b -> p (a b)"), in0=ps[:, b, 0:384],
                                  in1=swb_sb[:, 0:3, :].rearrange("p a b -> p (a b)"), op=ALU.add),
                         reads=[("S", b), ("swb", s)], writes=[("sadd", u, ri) for ri in range(3)])
                else:
                    for ri, r in enumerate((-2, -1, 0)):
                        kb = qb + r
                        if kb < 0:
                            continue
                        tile_i = ri
                        if qb == 0 and r == 0:
                            tile_i = 3
                        if qb == 1 and r == -1:
                            tile_i = 4
                        k.op("dve", I("tensor_tensor", out=sadd[u][:, ri, :], in0=ps[:, b, ri * 128:(ri + 1) * 128],
                                      in1=swb_sb[:, tile_i, :], op=ALU.add),
                             reads=[("S", b), ("swb", s)], writes=[("sadd", u, ri)])
                if qb + 1 < NB and qb >= 1:
                    k.op("dve", I("tensor_tensor", out=ssm2[u][0:16, :], in0=ps[0:16, b2, 0:256],
                                  in1=swsm_sb[0:16, :, :].rearrange("p a b -> p (a b)"), op=ALU.add),
                         reads=[("Sn", b2), ("Sm", b2), ("swsm", s)], writes=[("ssm", u), ("ssm", 2 + u)])
                elif qb + 1 < NB:
                    k.op("dve", I("tensor_tensor", out=ssm2[u][0:16, 0:128], in0=ps[0:16, b2, 0:128], in1=swsm_sb[0:16, 0, :], op=ALU.add),
                         reads=[("Sn", b2), ("swsm", s)], writes=[("ssm", u)])
                elif qb >= 1:
                    k.op("dve", I("tensor_tensor", out=ssm2[u][0:16, 128:256], in0=ps[0:16, b2, 128:256], in1=swsm_sb[0:16, 1, :], op=ALU.add),
                         reads=[("Sm", b2), ("swsm", s)], writes=[("ssm", 2 + u)])

            def stB2(i, it):
                qb = it["qb"]
                u = i % 2
                ris = [ri for ri, r in enumerate((-2, -1, 0)) if qb + r >= 0]
                lo, hi = ris[0], ris[-1] + 1
                k.op("act", I("activation", out=E_b[u][:, lo * 128:hi * 128], in_=sadd[u][:, lo:hi, :].rearrange("p a b -> p (a b)"),
                              func=AF.Exp, scale=0.125),
                     reads=[("sadd", u, ri) for ri in ris], writes=[("E", u, ri) for ri in ris])
                if qb + 1 < NB:
                    k.op("act", I("activation", out=Esm[u][0:16, :], in_=ssm2[u][0:16, 0:128], func=AF.Exp, scale=0.125),
                         reads=[("ssm", u)], writes=[("Esm", u)])
                if qb >= 1:
                    k.op("act", I("activation", out=Esm[2 + u][0:16, :], in_=ssm2[u][0:16, 128:256], func=AF.Exp, scale=0.125,
                                  bias=metab_sb[0:16, h, qb:qb + 1]),
                         reads=[("ssm", 2 + u), "metab"], writes=[("Esm", 2 + u)])

            def stC(i, it):
                qb = it["qb"]
                u = i % 2
                acc = 4 + i % 2
                mms = []
                for ri, r in enumerate((-2, -1, 0)):
                    kb = qb + r
                    if kb < 0:
                        continue
                    mms.append((E_b[u][:, ri * 128:(ri + 1) * 128], v[:, kb, :], [("E", u, ri)]))
                if qb + 1 < NB:
                    mms.append((Esm[u][0:16, :], v[0:16, qb + 1, :], [("Esm", u)]))
                if qb >= 1:
                    mms.append((Esm[2 + u][0:16, :], v[0:16, 0, :], [("Esm", 2 + u)]))
                for mi, (l_, r_, rk_) in enumerate(mms):
                    k.op("pe", I("matmul", ps[:, acc, 0:65], lhsT=l_, rhs=r_, start=(mi == 0), stop=(mi == len(mms) - 1)),
                         reads=rk_ + [("V", s)], writes=[("acc", acc)])

            def stD(i, it):
                qb = it["qb"]
                acc = 4 + i % 2
                ri_ = cnt["rd"] % 8
                cnt["rd"] += 1
                k.op("dve", I("tensor_tensor", out=rden[ri_], in0=ps[:, acc, 64:65], in1=esink[:, h:h + 1], op=ALU.add),
                     reads=[("acc", acc), "esink"], writes=[("rden", ri_)])
                k.op("dve", I("reciprocal", out=rden[ri_], in_=rden[ri_]), reads=[("rden", ri_)], writes=[("rden", ri_)])
                k.op("dve", I("scalar_tensor_tensor", out=mixg[:, h // 2, qb, (h % 2) * 64:(h % 2) * 64 + 64], in0=ps[:, acc, 0:64], scalar=rden[ri_], in1=g[:, qb, :],
                                                             op0=ALU.mult, op1=ALU.mult),
                     reads=[("acc", acc), ("rden", ri_), ("G", s)], writes=[("mixg", h, qb)])

            pipeline(k, items, [stA, stB, stB2, stC, stD], [0, 1, 2, 3, 4])

        load_head(0, 0)
        for h in range(16):
            s = h % 2
            if h + 1 < 16:
                load_head(h + 1, (h + 1) % 2)
            if even:
                if h < 8:
                    sb_head(h, s)
                else:
                    kq_head(h, s, "mla")
            else:
                if h < 8:
                    swa_head(h, s)
                else:
                    kq_head(h, s, "fox")
        k.barrier()
        if stop == ("B", layer):
            mixdbg = nc.dram_tensor("mixdbg", [128, 8, NB, 128], BF16, kind="ExternalOutput").ap()
            k.dma("sp", mixdbg, mixg, writes=["mixdbg"])
            k.barrier()
            k.emit()
            return nc

        _cb = Bump(work, WORK)
        hb3 = [_cb.f32([128, D]) for _ in range(5)]
        mT2 = [_cb.bf16([128, 8, 128]) for _ in range(2)]
        stats2 = [_cb.f32([128, 2, 6]) for _ in range(2)]
        gainb = _cb.f32([128, D])
        biasb = _cb.f32([128, D])
        k.dma("sp", gainb, ln_gain[layer:layer + 1, :].partition_broadcast(128), writes=["gainb"])
        k.dma("sp", biasb, ln_bias[layer:layer + 1, :].partition_broadcast(128), writes=["biasb"])
        def c0(i, tb):
            u3 = tb % 5
            k.dma("sp", hb3[u3], hbuf[tb * 128:(tb + 1) * 128, :], writes=[("hblk", u3)])
            pb0 = psb_bf(0)
            for c in range(8):
                k.op("pe", I("transpose", out=pb0[:, c * 128:(c + 1) * 128], in_=mixg[:, c, tb, :], identity=ident_b[:]),
                     reads=[("mixg", 2 * c, tb), ("mixg", 2 * c + 1, tb), "ident_b"], writes=[("pmT", 0)])
            evac(mT2[tb % 2], pb0.rearrange("p (c t) -> p c t", t=128), reads=[("pmT", 0)], writes=[("mT", tb % 2)])

        def c1(i, tb):
            yb = 1 + 2 * (tb % 2)
            for half in range(2):
                for c in range(8):
                    k.op("pe", I("matmul", ps[:, yb + half, :], lhsT=mT2[tb % 2][:, c, :], rhs=wout_sb[:, c, half * 512:(half + 1) * 512],
                                 start=(c == 0), stop=(c == 7)),
                         reads=[("mT", tb % 2), ("wout", c)], writes=[("py", yb + half)])

        def c2(i, tb):
            u3 = tb % 5
            yb = 1 + 2 * (tb % 2)
            z = hb3[u3]
            sm = small[:, 32 + 8 * (tb % 2):40 + 8 * (tb % 2)]
            st_ = stats2[tb % 2]
            for half in range(2):
                k.op("dve", I("scalar_tensor_tensor", out=z[:, half * 512:(half + 1) * 512], in0=z[:, half * 512:(half + 1) * 512],
                              scalar=DN_ALPHA, in1=ps[:, yb + half, :], op0=ALU.mult, op1=ALU.add),
                     reads=[("hblk", u3), ("py", yb + half)], writes=[("hblk", u3)])
                k.op("dve", I("bn_stats", out=st_[:, half, :], in_=z[:, half * 512:(half + 1) * 512]),
                     reads=[("hblk", u3)], writes=[("stats", tb % 2, half)])
            k.op("dve", I("bn_aggr", out=sm[:, 0:2], in_=st_.rearrange("p a b -> p (a b)")),
                 reads=[("stats", tb % 2, 0), ("stats", tb % 2, 1)], writes=[("mv", tb % 2)])
            k.op("act", I("activation", out=sm[:, 2:3], in_=sm[:, 1:2], func=AF.Ln, bias=LN_EPS), reads=[("mv", tb % 2)], writes=[("lnv", tb % 2)])
            k.op("act", I("activation", out=sm[:, 3:4], in_=sm[:, 2:3], func=AF.Exp, scale=-0.5), reads=[("lnv", tb % 2)], writes=[("rstd2", tb % 2)])

        def c3(i, tb):
            u3 = tb % 5
            z = hb3[u3]
            sm = small[:, 32 + 8 * (tb % 2):40 + 8 * (tb % 2)]
            k.op("dve", I("tensor_scalar", out=z, in0=z, scalar1=sm[:, 0:1], scalar2=sm[:, 3:4], op0=ALU.subtract, op1=ALU.mult),
                 reads=[("hblk", u3), ("mv", tb % 2), ("rstd2", tb % 2)], writes=[("hblk", u3)])
            k.op("pool", I("tensor_tensor", out=z, in0=z, in1=gainb, op=ALU.mult), reads=[("hblk", u3), "gainb"], writes=[("hblk", u3)])
            k.op("pool", I("tensor_tensor", out=z, in0=z, in1=biasb, op=ALU.add), reads=[("hblk", u3), "biasb"], writes=[("hblk", u3)])

        def c4(i, tb):
            u3 = tb % 5
            z = hb3[u3]
            if last:
                lo = max(tb * 128, 16)
                hi_ = min((tb + 1) * 128, 16 + S)
                if lo < hi_:
                    k.dma("sp", out[lo - 16:hi_ - 16, :], z[lo - tb * 128:hi_ - tb * 128, :], reads=[("hblk", u3)], writes=[("out", tb)])
            else:
                k.dma("sp", hbuf[tb * 128:(tb + 1) * 128, :], z, reads=[("hblk", u3)], writes=[("hbuf", tb)])
                for c in range(8):
                    k.op("pe", I("transpose", out=ps[:, 6 + c // 4, (c % 4) * 128:(c % 4 + 1) * 128],
                                 in_=z[:, c * 128:(c + 1) * 128], identity=ident_f[:]),
                         reads=[("hblk", u3), "ident_f"], writes=[("psT", c // 4)])
                evac(hT[:, :, tb * 128:(tb + 1) * 128], ps[:, 6:8, :].rearrange("p a (c t) -> p (a c) t", t=128),
                     reads=[("psT", 0), ("psT", 1)], writes=[("hT", tb)], eng="act")

        pipeline(k, list(range(NB)), [c0, c1, c2, c3, c4], [0, 1, 2, 3, 4])
        k.barrier()

    k.emit()
    return nc


_CACHE = {}


def run(inputs, NB, S, depth, n_cores):
    key = (NB, S, depth)
    if key not in _CACHE:
        _CACHE[key] = (build(NB, S, depth), make_consts(NB))
    nc, consts = _CACHE[key]
    n_even = (depth + 1) // 2
    n_odd = depth // 2
    shared = {}
    for name in ("meta_tokens", "g_cq", "g_ckv", "w_uq", "w_ukv", "b_forget", "sink_logits", "ln_gain", "ln_bias"):
        shared[name] = np.ascontiguousarray(np.asarray(inputs[name], dtype=np.float32))
    for name in ("w_in_even", "w_out_even"):
        shared[name] = np.ascontiguousarray(np.asarray(inputs[name], dtype=np.float32)[:n_even])
    for name in ("w_in_odd", "w_out_odd"):
        shared[name] = np.ascontiguousarray(np.asarray(inputs[name], dtype=np.float32)[:max(n_odd, 1)])
    for name in ("g_cq", "g_ckv", "w_uq", "w_ukv"):
        shared[name] = shared[name][:n_even]
    for name in ("b_forget", "sink_logits"):
        shared[name] = shared[name][:max(n_odd, 1)]
    shared["ln_gain"] = shared["ln_gain"][:depth]
    shared["ln_bias"] = shared["ln_bias"][:depth]
    for n, _ in CONST_SPECS:
        shared["c_" + n] = consts[n]
    x = np.asarray(inputs["x"], dtype=np.float32)
    in_maps = []
    for b in range(n_cores):
        m = dict(shared)
        m["x"] = np.ascontiguousarray(x[b])
        in_maps.append(m)
    res = run_bass_kernel_spmd(nc, in_maps, core_ids=list(range(n_cores)))
    return np.stack([np.asarray(r["out"]) for r in res.results], axis=0).astype(np.float32)


def kernel(**inputs):
    x = np.asarray(inputs["x"])
    B, S, _ = x.shape
    n = S + 16
    NB = -(-n // 128)
    return run(inputs, NB, S, 4, B)
```

```python
import contextlib
import math
import numpy as np
import concourse.bass as bass
import concourse.mybir as mybir
from concourse.bass_utils import run_bass_kernel_spmd

F32 = mybir.dt.float32
BF16 = mybir.dt.bfloat16
AF = mybir.ActivationFunctionType
ALU = mybir.AluOpType

ENGS = ("pe", "act", "dve", "pool", "sp")
NDMA = 8

D = 1024
HD = 64
NEGM = -30000.0
DN_ALPHA = 8.0 ** 0.25
LN_EPS = 1e-5
RMS_EPS = 1e-6
EVEN_IN = 3104
ODD_IN = 3336


class K:
    def __init__(self, nc):
        self.nc = nc
        self.stack = contextlib.ExitStack()
        self.prog = {e: [] for e in ENGS}
        self.cnt = {e: 0 for e in ENGS}
        self.seen = {e: {} for e in ENGS}
        self.lastw = {}
        self.readers = {}
        self.sem = {}
        for e in ENGS:
            self.sem[("E", e)] = self.stack.enter_context(nc.semaphore("s_" + e))
        self.dma_n = {"sp": 0, "pool": 0, "act": 0}
        self.dma_val = {}
        for q in ("sp", "pool", "act"):
            for i in range(NDMA):
                self.sem[("D", q, i)] = self.stack.enter_context(nc.semaphore(f"d_{q}{i}"))
                self.dma_val[(q, i)] = 0

    def sbuf(self, name, shape, dtype):
        return self.stack.enter_context(self.nc.sbuf_tensor(name, list(shape), dtype))

    def psum(self, name, shape, dtype):
        return self.stack.enter_context(self.nc.psum_tensor(name, list(shape), dtype))

    def _deps(self, eng, reads, writes):
        deps = {}

        def add(ev):
            if ev is None:
                return
            k, v = ev
            if k == ("E", "pe") and eng == "pe":
                return
            if deps.get(k, 0) < v:
                deps[k] = v

        for r in reads:
            add(self.lastw.get(r))
        for w in writes:
            add(self.lastw.get(w))
            for ev in self.readers.get(w, ()):
                add(ev)
        waits = []
        seen = self.seen[eng]
        for k, v in deps.items():
            if seen.get(k, 0) < v:
                seen[k] = v
                waits.append((k, v))
        return waits

    def _commit(self, ev, reads, writes):
        for w in writes:
            self.lastw[w] = ev
            self.readers[w] = []
        for r in reads:
            self.readers.setdefault(r, []).append(ev)

    def op(self, eng, fn, reads=(), writes=()):
        waits = self._deps(eng, reads, writes)
        self.cnt[eng] += 1
        ev = (("E", eng), self.cnt[eng])
        self.prog[eng].append((waits, fn, ev[0], 1))
        self._commit(ev, reads, writes)

    def dma(self, q, out, in_, reads=(), writes=(), **kw):
        waits = self._deps(q, reads, writes)
        n = self.dma_n[q]
        self.dma_n[q] += 1
        slot = n % NDMA
        key = ("D", q, slot)
        prev = self.dma_val[(q, slot)]
        if prev and self.seen[q].get(key, 0) < prev:
            self.seen[q][key] = prev
            waits.append((key, prev))
        val = prev + 16
        self.dma_val[(q, slot)] = val
        fn = I("dma_start", out=out, in_=in_, **kw)
        self.prog[q].append((waits, fn, key, 16))
        self._commit((key, val), reads, writes)

    def barrier(self):
        evs = []
        for (q, i), v in self.dma_val.items():
            if v:
                evs.append((("D", q, i), v))
        for e in ENGS:
            if self.cnt[e]:
                evs.append((("E", e), self.cnt[e]))
        for eng in ENGS:
            waits = []
            for k, v in evs:
                if k == ("E", eng):
                    continue
                if self.seen[eng].get(k, 0) < v:
                    self.seen[eng][k] = v
                    waits.append((k, v))
            if waits:
                self.prog[eng].append((waits, None, None, 0))
        self.lastw = {}
        self.readers = {}

    def emit(self):
        nc = self.nc
        sem = self.sem

        def replay(name):
            def run(e):
                for waits, fn, key, inc in self.prog[name]:
                    for k, v in waits:
                        e.wait_ge(sem[k], v)
                    if fn is not None:
                        fn(e).then_inc(sem[key], inc)
            return run

        with nc.Block() as block:
            block.tensor(replay("pe"))
            block.scalar(replay("act"))
            block.vector(replay("dve"))
            block.gpsimd(replay("pool"))
            block.sync(replay("sp"))
        self.stack.close()


def I(name, *args, **kw):
    return lambda e: getattr(e, name)(*args, **kw)


def pipeline(k, items, stages, lags=None):
    n = len(items)
    if lags is None:
        lags = list(range(len(stages)))
    for s in range(n + max(lags)):
        for st, lag in zip(stages, lags):
            i = s - lag
            if 0 <= i < n:
                st(i, items[i])


def _chunk_of(p):
    return 0 if p < 16 else 1 + (p - 16) // 64


def make_consts(NB):
    import ml_dtypes
    bf = ml_dtypes.bfloat16
    NT = NB * 128
    c = {}
    c["ident"] = np.eye(128, dtype=np.float32)
    l = np.arange(128)
    c["maskS"] = np.where(l[None, :] < l[:, None], 0.0, NEGM).astype(bf)
    c["maskF"] = np.where(l[:, None] <= l[None, :], 0.0, NEGM).astype(bf)
    cl = np.array([_chunk_of(128 + i) for i in range(128)])
    c["maskM"] = np.where(cl[:, None] <= cl[None, :], 0.0, NEGM).astype(bf)
    mn = np.full((128, 128), NEGM, np.float32)
    mn[:16, 80:] = 0.0
    c["maskN"] = mn.astype(bf)
    half = 16
    inv = 10000.0 ** (-np.arange(half, dtype=np.float32) / half)
    pos = np.arange(NT, dtype=np.float32)
    ang = pos[:, None] * inv[None, :]
    c["cos"] = np.cos(ang).astype(np.float32).reshape(NB, 128, 16).transpose(1, 0, 2).copy()
    c["sin"] = np.sin(ang).astype(np.float32).reshape(NB, 128, 16).transpose(1, 0, 2).copy()
    slopes = np.array([2.0 ** (-8.0 * (h + 1) / 8) for h in range(8)], np.float32)
    sw = np.zeros((8, 128, 5, 128), np.float32)
    QB = 4
    tq = QB * 128 + l

    def vis_band(s, t):
        cs, ct = _chunk_of(s), _chunk_of(t)
        return (s >= 16) and (cs <= ct) and (ct - cs <= 2)

    for ri, r in enumerate((-2, -1, 0)):
        ks = (QB + r) * 128 + l
        vis = np.array([[vis_band(s, t) for t in tq] for s in ks])
        dist = np.abs(tq[None, :] - ks[:, None]).astype(np.float32)
        for h in range(8):
            sw[h, :, ri, :] = np.where(vis, -8.0 * slopes[h] * dist, NEGM)
    vis = np.array([[(s < 16) or vis_band(s, t) for t in l] for s in l])
    dist = np.abs(l[None, :] - l[:, None]).astype(np.float32)
    for h in range(8):
        sw[h, :, 3, :] = np.where(vis, -8.0 * slopes[h] * dist, NEGM)
    t1 = 128 + l
    vis = np.array([[vis_band(s, t) for t in t1] for s in l])
    dist = np.abs(t1[None, :] - l[:, None]).astype(np.float32)
    for h in range(8):
        sw[h, :, 4, :] = np.where(vis, -8.0 * slopes[h] * dist, NEGM)
    c["swb"] = sw
    sm = np.zeros((8, 16, 2, 128), np.float32)
    kn = (QB + 1) * 128 + np.arange(16)
    vis = np.array([[vis_band(s, t) for t in tq] for s in kn])
    dist = np.abs(tq[None, :] - kn[:, None]).astype(np.float32)
    for h in range(8):
        sm[h, :, 0, :] = np.where(vis, -8.0 * slopes[h] * dist, NEGM)
        sm[h, :, 1, :] = -8.0 * slopes[h] * (l[None, :] - np.arange(16)[:, None]).astype(np.float32)
    c["swsm"] = sm
    mb = np.zeros((16, 8, NB), np.float32)
    for h in range(8):
        for qb in range(NB):
            mb[:, h, qb] = -slopes[h] * 128.0 * qb
    c["metab"] = mb
    return c


CONST_SPECS = [("ident", F32), ("maskS", BF16), ("maskF", BF16), ("maskM", BF16), ("maskN", BF16),
               ("cos", F32), ("sin", F32), ("swb", F32), ("swsm", F32), ("metab", F32)]


def build(NB, S, depth, stop=None, debug=False):
    NT = NB * 128
    NPAD = NT - 16 - S
    assert NPAD >= 0
    NG = (NB + 3) // 4
    nc = bass.Bass("TRN2", target_bir_lowering=False)
    k = K(nc)
    n_even = (depth + 1) // 2
    n_odd = depth // 2

    def din(name, shape, dt=F32):
        return nc.dram_tensor(name, list(shape), dt, kind="ExternalInput").ap()

    def dscr(name, shape, dt):
        return nc.dram_tensor(name, list(shape), dt, kind=("ExternalOutput" if debug else "Internal")).ap()

    x = din("x", [S, D])
    meta = din("meta_tokens", [16, D])
    w_in_even = din("w_in_even", [n_even, D, EVEN_IN])
    g_cq = din("g_cq", [n_even, 256])
    g_ckv = din("g_ckv", [n_even, 256])
    w_uq = din("w_uq", [n_even, 256, 768])
    w_ukv = din("w_ukv", [n_even, 256, 1024])
    w_out_even = din("w_out_even", [n_even, D, D])
    w_in_odd = din("w_in_odd", [max(n_odd, 1), D, ODD_IN])
    b_forget = din("b_forget", [max(n_odd, 1), 8])
    sink_logits = din("sink_logits", [max(n_odd, 1), 8])
    w_out_odd = din("w_out_odd", [max(n_odd, 1), D, D])
    ln_gain = din("ln_gain", [depth, D])
    ln_bias = din("ln_bias", [depth, D])
    cshape = {"ident": [128, 128], "maskS": [128, 128], "maskF": [128, 128], "maskM": [128, 128],
              "maskN": [128, 128], "cos": [128, NB, 16], "sin": [128, NB, 16],
              "swb": [8, 128, 5, 128], "swsm": [8, 16, 2, 128], "metab": [16, 8, NB]}
    cd = {n: din("c_" + n, cshape[n], dt) for n, dt in CONST_SPECS}
    out = nc.dram_tensor("out", [S, D], F32, kind="ExternalOutput").ap()

    hbuf = dscr("hbuf", [NT, D], F32)
    QTd_ = dscr("QTs", [16, 96, NT], BF16)
    KTd_ = dscr("KTs", [16, 96, NT], BF16)
    Vd_ = dscr("Vs", [16, 128, NB, 65], BF16)
    Gd_ = dscr("Gs", [16, 128, NB, 64], F32)
    zscr = dscr("zscr", [128, D], F32)

    hT = k.sbuf("hT", [128, 8, NT], BF16)
    PA_WORDS = 6200 + NT + 2 * NB * 16
    MW = max(NB * 512, PA_WORDS)
    mreg = k.sbuf("mreg", [128, MW], F32)
    mixg = mreg[:, 0:NB * 512].bitcast(BF16).rearrange("p (h b c) -> p h b c", h=8, b=NB)
    ident_f = k.sbuf("ident_f", [128, 128], F32)
    ident_b = k.sbuf("ident_b", [128, 128], BF16)
    mS = k.sbuf("mS", [128, 128], BF16)
    mF = k.sbuf("mF", [128, 128], BF16)
    mM = k.sbuf("mM", [128, 128], BF16)
    mN = k.sbuf("mN", [128, 128], BF16)
    gq_b = k.sbuf("gq_b", [128, 512], F32)
    wslab = k.sbuf("wslab", [128, 2, 8, 544], BF16)
    wuq_sb = k.sbuf("wuq", [128, 2, 768], BF16)
    wukv_sb = k.sbuf("wukv", [128, 2, 1024], BF16)
    WORK = 8700
    work = k.sbuf("work", [128, WORK], F32)
    small = k.sbuf("small", [128, 64], F32)
    ones_f = k.sbuf("ones_f", [128, 512], F32)
    metab_sb = k.sbuf("metab_sb", [16, 8, NB], F32)
    esink = k.sbuf("esink", [128, 8], F32)
    negcum = k.sbuf("negcum", [128, NB, 8], F32)
    crefB = k.sbuf("crefB", [128, 8, NG], F32)
    ps = k.psum("ps", [128, 8, 512], F32)

    class Bump:
        def __init__(self, reg, size):
            self.reg, self.size, self.off = reg, size, 0

        def _take(self, words):
            o = self.off
            self.off += words
            assert self.off <= self.size, (self.off, self.size)
            return o

        def f32(self, shape):
            n = int(np.prod(shape[1:]))
            o = self._take(n)
            ap = self.reg[:shape[0], o:o + n]
            if len(shape) == 3:
                ap = ap.rearrange("p (a b) -> p a b", b=shape[2])
            return ap

        def bf16(self, shape):
            n = int(np.prod(shape[1:]))
            w = (n + 1) // 2
            o = self._take(w)
            ap = self.reg[:shape[0], o:o + w].bitcast(BF16)[:, 0:n]
            if len(shape) == 3:
                ap = ap.rearrange("p (a b) -> p a b", b=shape[2])
            return ap

    hT_flat = hT[:].rearrange("p c t -> p (c t)")
    wflat = wslab[:].rearrange("p a c n -> p (a c n)")

    def psb(bank):
        return ps[:, bank, :]

    def psb_bf(bank):
        return ps[:, bank, :].bitcast(BF16)

    evac_rr = [0]

    def evac(out_ap, in_ap, reads, writes, scale=None, eng=None):
        if eng is None:
            eng = "act" if evac_rr[0] % 2 == 0 else "dve"
            evac_rr[0] += 1
        if eng == "act":
            if scale is None:
                k.op("act", I("copy", out=out_ap, in_=in_ap), reads=reads, writes=writes)
            else:
                k.op("act", I("mul", out=out_ap, in_=in_ap, mul=scale), reads=reads, writes=writes)
        else:
            if scale is None:
                k.op("dve", I("tensor_copy", out=out_ap, in_=in_ap), reads=reads, writes=writes)
            else:
                k.op("dve", I("tensor_scalar_mul", out=out_ap, in0=in_ap, scalar1=scale), reads=reads, writes=writes)

    k.dma("sp", ident_f[:], cd["ident"], writes=["ident_f"])
    k.dma("sp", mS[:], cd["maskS"], writes=["mS"])
    k.dma("sp", mF[:], cd["maskF"], writes=["mF"])
    k.dma("sp", mM[:], cd["maskM"], writes=["mM"])
    k.dma("sp", mN[:], cd["maskN"], writes=["mN"])
    k.dma("sp", metab_sb[:], cd["metab"], writes=["metab"])
    k.op("dve", I("tensor_copy", out=ident_b[:], in_=ident_f[:]), reads=["ident_f"], writes=["ident_b"])
    k.op("dve", I("memset", ones_f[:], 1.0), writes=["ones_f"])
    k.op("pool", I("memset", mreg[:, 0:D], 0.0), writes=["wz"])
    k.dma("sp", hbuf[0:16, :], meta, writes=["hb_meta"])
    npieces = 8
    step = (S + npieces - 1) // npieces
    for i in range(npieces):
        a, b = i * step, min(S, (i + 1) * step)
        if a < b:
            k.dma("sp", hbuf[16 + a:16 + b, :], x[a:b, :], writes=[("hb_x", i)])
    if NPAD > 0:
        k.dma("sp", hbuf[16 + S:NT, :], mreg[0:NPAD, 0:D], reads=["wz"], writes=["hb_pad"])

    _wb = Bump(work, WORK)
    _hblks = [_wb.f32([128, D]) for _ in range(2)]

    def hblk(i):
        return _hblks[i]

    def make_hT(tb, src, src_key):
        for c in range(8):
            k.op("pe", I("transpose", out=ps[:, 6 + c // 4, (c % 4) * 128:(c % 4 + 1) * 128],
                                                  in_=src[:, c * 128:(c + 1) * 128], identity=ident_f[:]),
                 reads=[src_key, "ident_f"], writes=[("psT", c // 4)])
        o = hT[:, :, tb * 128:(tb + 1) * 128]
        i_ = ps[:, 6:8, :].rearrange("p a (c t) -> p (a c) t", t=128)
        evac(o, i_, reads=[("psT", 0), ("psT", 1)], writes=[("hT", tb)], eng="act")

    for tb in range(NB):
        hb = hblk(tb % 2)
        kk = ("hblk", tb % 2)
        if tb == 0:
            k.dma("sp", hb[0:16, :], meta, writes=[kk])
            k.dma("sp", hb[16:128, :], x[0:112, :], writes=[kk])
        else:
            lo = tb * 128 - 16
            hi = min(S, lo + 128)
            n = hi - lo
            if n < 128:
                k.op("pool", I("memset", hb, 0.0), writes=[kk])
            if n > 0:
                k.dma("sp", hb[0:n, :], x[lo:hi, :], writes=[kk])
        make_hT(tb, hb, kk)
    k.barrier()
    if stop == "P":
        k.emit()
        return nc

    slab_state = {"n": 0}

    def load_slab(w_ap, c0, ncols):
        buf = slab_state["n"] % 2
        slab_state["n"] += 1
        src = w_ap.rearrange("(kc p) c -> p kc c", p=128)
        for kc in range(8):
            k.dma("pool", wslab[:, buf, kc, 0:ncols], src[:, kc, c0:c0 + ncols], writes=[("slab", buf, kc)])
        return buf

    tgroups = []
    t0 = 0
    while t0 < NT:
        w = min(512, NT - t0)
        tgroups.append((t0, w))
        t0 += w

    pa_rr = [0]

    def proj_fm(buf, col0, npairs, dst, heads, scale=None, rows=64):
        for p in range(npairs):
            for (t0, W) in tgroups:
                bank = pa_rr[0] % 4
                pa_rr[0] += 1
                for kc in range(8):
                    k.op("pe", I("matmul",
                        ps[:, bank, 0:W], lhsT=wslab[:, buf, kc, col0 + p * 128:col0 + (p + 1) * 128],
                        rhs=hT[:, kc, t0:t0 + W], start=(kc == 0), stop=(kc == 7)),
                        reads=[("slab", buf, kc)] + [("hT", t0 // 128 + j) for j in range(W // 128)],
                        writes=[("bk", bank)])
                st = stage_fm[bank % 2]
                evac(st[:, 0:W], ps[:, bank, 0:W], reads=[("bk", bank)], writes=[("stfm", bank % 2)], scale=scale)
                for hh in range(2):
                    h = heads[2 * p + hh]
                    if h is None:
                        continue
                    k.dma("sp", dst[h, 0:64, t0:t0 + W], st[hh * 64:(hh + 1) * 64, 0:W],
                          reads=[("stfm", bank % 2)], writes=[("dst", id(dst), h, t0)])

    _ab = Bump(mreg, MW)
    stage_fm = [_ab.bf16([128, 512]) for _ in range(2)]
    stage_v = [_ab.bf16([128, 8, 65]) for _ in range(2)]
    stage_g = [_ab.f32([128, 512]) for _ in range(2)]
    cn_bf = _ab.bf16([128, 512])
    cT_sb = _ab.bf16([128, 4, 128])
    qtm = _ab.bf16([128, 8, 96])
    ktm = _ab.bf16([128, 8, 96])
    qkT_sb = [_ab.bf16([128, 8, 128]) for _ in range(2)]
    rtmp = _ab.f32([128, 4, 128])
    krope = _ab.bf16([128, 32])
    sq_junk = _ab.f32([128, 256])
    fz_sb = _ab.f32([128, NT])
    fz_e = _ab.f32([128, 512])
    cosT = _ab.f32([128, NB, 16])
    sinT = _ab.f32([128, NB, 16])

    def proj_tm_block(buf, tb, bank, col0, ncols, nbanks=1):
        for kc in range(8):
            for bb in range(nbanks):
                c_lo = col0 + bb * 512
                c_n = min(512, ncols - bb * 512)
                k.op("pe", I("matmul",
                    ps[:, bank + bb, 0:c_n], lhsT=hT[:, kc, tb * 128:(tb + 1) * 128],
                    rhs=wslab[:, buf, kc, c_lo:c_lo + c_n], start=(kc == 0), stop=(kc == 7)),
                    reads=[("slab", buf, kc), ("hT", tb)], writes=[("bk", bank + bb)])

    def proj_v(buf, col0, nh, hbase, ones_col=True):
        for tb in range(NB):
            bank = pa_rr[0] % 4
            pa_rr[0] += 1
            proj_tm_block(buf, tb, bank, col0, nh * 64)
            st = stage_v[bank % 2]
            evac(st[:, 0:nh, 0:64], ps[:, bank, 0:nh * 64].rearrange("p (h c) -> p h c", c=64),
                 reads=[("bk", bank)], writes=[("stv", bank % 2)])
            k.dma("sp", Vd_[hbase:hbase + nh, :, tb, :].rearrange("h p c -> p h c"), st[:, 0:nh, :],
                  reads=[("stv", bank % 2)], writes=[("Vd", hbase, tb)])

    def proj_gate(buf, col0, hbase):
        for tb in range(NB):
            bank = pa_rr[0] % 4
            pa_rr[0] += 1
            proj_tm_block(buf, tb, bank, col0, 512)
            st = stage_g[bank % 2]
            k.op("act", I("activation", out=st, in_=ps[:, bank, :], func=AF.Silu),
                 reads=[("bk", bank)], writes=[("stg", bank % 2)])
            k.dma("sp", Gd_[hbase:hbase + 8, :, tb, :].rearrange("h p c -> p h c"),
                  st.rearrange("p (h c) -> p h c", c=64), reads=[("stg", bank % 2)], writes=[("Gd", hbase, tb)])

    def rope_ops(x1, x2, o1, o2, cs, sn, shape_key):
        t = [rtmp[:, i, 0:int(np.prod(x1.shape[1:]))] for i in range(4)]
        if len(x1.shape) == 3:
            t = [ti.rearrange("p (a b) -> p a b", b=x1.shape[2]) for ti in t]
        rk = ["rt0", "rt1", "rt2", "rt3"]
        k.op("dve", I("tensor_tensor", out=t[0], in0=x1, in1=cs, op=ALU.mult), reads=shape_key, writes=[rk[0]])
        k.op("dve", I("tensor_tensor", out=t[1], in0=x2, in1=sn, op=ALU.mult), reads=shape_key, writes=[rk[1]])
        k.op("dve", I("tensor_tensor", out=t[2], in0=x2, in1=cs, op=ALU.mult), reads=shape_key, writes=[rk[2]])
        k.op("dve", I("tensor_tensor", out=t[3], in0=x1, in1=sn, op=ALU.mult), reads=shape_key, writes=[rk[3]])
        return t, rk

    for layer in range(depth):
        li = layer // 2
        even = (layer % 2 == 0)
        last = (layer == depth - 1)
        w_in = (w_in_even if even else w_in_odd)[li]
        w_out = (w_out_even if even else w_out_odd)[li]

        k.op("pool", I("memset", stage_v[0][:, :, 64:65], 1.0), writes=[("stv", 0)])
        k.op("pool", I("memset", stage_v[1][:, :, 64:65], 1.0), writes=[("stv", 1)])

        if even:
            k.dma("sp", cosT, cd["cos"], writes=["cosT"])
            k.dma("sp", sinT, cd["sin"], writes=["sinT"])
            k.dma("sp", gq_b[:, 0:256], g_cq[li:li + 1, :].partition_broadcast(128), writes=["gq_b"])
            k.dma("sp", gq_b[:, 256:512], g_ckv[li:li + 1, :].partition_broadcast(128), writes=["gq_b2"])
            for kc in range(2):
                k.dma("pool", wuq_sb[:, kc, :], w_uq[li, kc * 128:(kc + 1) * 128, :], writes=[("wuq", kc)])
                k.dma("pool", wukv_sb[:, kc, :], w_ukv[li, kc * 128:(kc + 1) * 128, :], writes=[("wukv", kc)])
            b0 = load_slab(w_in, 0, 512)
            b1 = load_slab(w_in, 512, 512)
            proj_fm(b0, 0, 4, QTd_, list(range(8)), scale=0.125)
            b2 = load_slab(w_in, 1024, 512)
            proj_fm(b1, 0, 4, KTd_, list(range(8)))
            if stop == ("a", 1):
                k.barrier(); k.emit(); return nc
            b3 = load_slab(w_in, 1536, 544)
            proj_v(b2, 0, 8, 0)
            if stop == ("a", 2):
                k.barrier(); k.emit(); return nc
            b4 = load_slab(w_in, 2080, 512)
            for tb in range(NB):
                for kc in range(8):
                    k.op("pe", I("matmul", ps[:, 0, :], lhsT=hT[:, kc, tb * 128:(tb + 1) * 128],
                                                         rhs=wslab[:, b3, kc, 0:512], start=(kc == 0), stop=(kc == 7)),
                         reads=[("slab", b3, kc), ("hT", tb)], writes=[("bk", 0)])
                for kc in range(8):
                    k.op("pe", I("matmul", ps[:, 1, 0:32], lhsT=hT[:, kc, tb * 128:(tb + 1) * 128],
                                                         rhs=wslab[:, b3, kc, 512:544], start=(kc == 0), stop=(kc == 7)),
                         reads=[("slab", b3, kc), ("hT", tb)], writes=[("bk", 1)])
                for j in range(2):
                    k.op("act", I("activation", out=sq_junk, in_=ps[:, 0, j * 256:(j + 1) * 256], func=AF.Square,
                                                            accum_out=small[:, j:j + 1]),
                         reads=[("bk", 0)], writes=["sqj", ("ssq", j)])
                k.op("act", I("activation", out=small[:, 2:4], in_=small[:, 0:2], func=AF.Ln, scale=1.0 / 256, bias=RMS_EPS),
                     reads=[("ssq", 0), ("ssq", 1)], writes=["lnms"])
                k.op("act", I("activation", out=small[:, 4:6], in_=small[:, 2:4], func=AF.Exp, scale=-0.5),
                     reads=["lnms"], writes=["rstd"])
                if stop == ("m", 1):
                    k.barrier(); k.emit(); return nc
                for j in range(2):
                    k.op("dve", I("scalar_tensor_tensor",
                        out=cn_bf[:, j * 256:(j + 1) * 256], in0=ps[:, 0, j * 256:(j + 1) * 256], scalar=small[:, 4 + j:5 + j],
                        in1=gq_b[:, j * 256:(j + 1) * 256], op0=ALU.mult, op1=ALU.mult),
                        reads=[("bk", 0), "rstd", "gq_b", "gq_b2"], writes=[("cn", j)])
                pbt = psb_bf(2)
                for j in range(4):
                    k.op("pe", I("transpose", out=pbt[:, j * 128:(j + 1) * 128], in_=cn_bf[:, j * 128:(j + 1) * 128],
                                                          identity=ident_b[:]),
                         reads=[("cn", j // 2), "ident_b"], writes=[("bk", 2)])
                evac(cT_sb, pbt[:, 0:512].rearrange("p (a b) -> p a b", b=128), reads=[("bk", 2)], writes=["cT"], eng="dve")
                if stop == ("m", 2):
                    k.barrier(); k.emit(); return nc
                for (bank, c0, cn_) in ((3, 0, 512), (4, 512, 256)):
                    for kc in range(2):
                        k.op("pe", I("matmul",
                            ps[:, bank, 0:cn_], lhsT=cT_sb[:, kc, :], rhs=wuq_sb[:, kc, c0:c0 + cn_], start=(kc == 0), stop=(kc == 1)),
                            reads=["cT", ("wuq", kc)], writes=[("bk", bank)])
                for (bank, c0) in ((5, 0), (6, 512)):
                    for kc in range(2):
                        k.op("pe", I("matmul",
                            ps[:, bank, :], lhsT=cT_sb[:, 2 + kc, :], rhs=wukv_sb[:, kc, c0:c0 + 512], start=(kc == 0), stop=(kc == 1)),
                            reads=["cT", ("wukv", kc)], writes=[("bk", bank)])
                if stop == ("m", 3):
                    k.barrier(); k.emit(); return nc
                qps = ps[:, 3:5, :].rearrange("p a b -> p (a b)")[:, 0:768].rearrange("p (h c) -> p h c", c=96)
                kvps = ps[:, 5:7, :].rearrange("p a b -> p (a b)").rearrange("p (h c) -> p h c", c=128)
                cs8 = cosT[:, tb, :].unsqueeze(1).broadcast_to([128, 8, 16])
                sn8 = sinT[:, tb, :].unsqueeze(1).broadcast_to([128, 8, 16])
                import os
                SK = os.environ.get("SK", "")
                if "a" not in SK:
                    t, rk = rope_ops(qps[:, :, 64:80], qps[:, :, 80:96], None, None, cs8, sn8, [("bk", 3), ("bk", 4), "cosT", "sinT"])
                if "b" not in SK:
                    k.op("dve", I("tensor_tensor", out=t[0], in0=t[0], in1=t[1], op=ALU.subtract), reads=rk[0:2], writes=[rk[0]])
                    k.op("dve", I("tensor_tensor", out=t[2], in0=t[2], in1=t[3], op=ALU.add), reads=rk[2:4], writes=[rk[2]])
                    k.op("act", I("copy", out=qtm[:, :, 64:80], in_=t[0]), reads=[rk[0]], writes=[("qtm", 1)])
                    k.op("act", I("copy", out=qtm[:, :, 80:96], in_=t[2]), reads=[rk[2]], writes=[("qtm", 2)])
                if "c" not in SK:
                    k.op("act", I("copy", out=qtm[:, :, 0:64], in_=qps[:, :, 0:64]), reads=[("bk", 3), ("bk", 4)], writes=[("qtm", 0)])
                if stop == ("m", 4):
                    k.barrier(); k.emit(); return nc
                krp = ps[:, 1, 0:32]
                t2, rk2 = rope_ops(krp[:, 0:16], krp[:, 16:32], None, None, cosT[:, tb, :], sinT[:, tb, :], [("bk", 1), "cosT", "sinT"])
                k.op("dve", I("tensor_tensor", out=t2[0], in0=t2[0], in1=t2[1], op=ALU.subtract), reads=rk2[0:2], writes=[rk2[0]])
                k.op("dve", I("tensor_tensor", out=t2[2], in0=t2[2], in1=t2[3], op=ALU.add), reads=rk2[2:4], writes=[rk2[2]])
                k.op("act", I("copy", out=ktm[:, :, 64:80], in_=t2[0].unsqueeze(1).broadcast_to([128, 8, 16])), reads=[rk2[0]], writes=[("ktm", 1)])
                k.op("act", I("copy", out=ktm[:, :, 80:96], in_=t2[2].unsqueeze(1).broadcast_to([128, 8, 16])), reads=[rk2[2]], writes=[("ktm", 2)])
                k.op("act", I("copy", out=ktm[:, :, 0:64], in_=kvps[:, :, 0:64]), reads=[("bk", 5), ("bk", 6)], writes=[("ktm", 0)])
                sv = stage_v[tb % 2]
                k.op("act", I("copy", out=sv[:, :, 0:64], in_=kvps[:, :, 64:128]), reads=[("bk", 5), ("bk", 6)], writes=[("stv", tb % 2)])
                k.dma("sp", Vd_[8:16, :, tb, :].rearrange("h p c -> p h c"), sv, reads=[("stv", tb % 2)], writes=[("Vd", 8, tb)])
                if stop == ("m", 5):
                    k.barrier(); k.emit(); return nc
                for which, src_t, dst_d in ((0, qtm, QTd_), (1, ktm, KTd_)):
                    pb7 = psb_bf(7)
                    for h in range(8):
                        k.op("pe", I("transpose", out=pb7[0:96, h * 128:(h + 1) * 128], in_=src_t[:, h, :],
                                                                                  identity=ident_b[:]),
                             reads=[((("qtm" if which == 0 else "ktm")), j) for j in range(3)] + ["ident_b"], writes=[("bk", 7)])
                    so = qkT_sb[which]
                    evac(so[0:96], pb7[0:96, :].rearrange("p (h t) -> p h t", t=128), reads=[("bk", 7)], writes=[("qkT", which)])
                    k.dma("sp", dst_d[8:16, :, tb * 128:(tb + 1) * 128].rearrange("h p t -> p h t"), so[0:96],
                          reads=[("qkT", which)], writes=[("dstm", which, tb)])
            if stop == ("a", 3):
                k.barrier(); k.emit(); return nc
            b5 = load_slab(w_in, 2592, 512)
            proj_gate(b4, 0, 0)
            proj_gate(b5, 0, 8)
        else:
            k.dma("sp", esink[:], sink_logits[li:li + 1, :].partition_broadcast(128), writes=["esink"])
            k.op("act", I("activation", out=esink[:], in_=esink[:], func=AF.Exp), reads=["esink"], writes=["esink"])
            k.dma("sp", small[0:8, 8:9], b_forget[li:li + 1, :].rearrange("a h -> h a"), writes=["bf"])
            k.op("dve", I("tensor_scalar_mul", out=small[0:8, 9:10], in0=small[0:8, 8:9], scalar1=-1.0), reads=["bf"], writes=["nbf"])
            b0 = load_slab(w_in, 0, 512)
            b1 = load_slab(w_in, 512, 256)
            proj_fm(b0, 0, 4, QTd_, list(range(8)))
            b2 = load_slab(w_in, 768, 512)
            for rep in range(4):
                proj_fm(b1, 0, 1, KTd_, [rep, 4 + rep])
            for tb in range(NB):
                bank = pa_rr[0] % 4
                pa_rr[0] += 1
                proj_tm_block(b1, tb, bank, 128, 128)
                st = stage_v[bank % 2]
                for rep in range(4):
                    evac(st[:, rep:rep + 5:4, 0:64], ps[:, bank, 0:128].rearrange("p (h c) -> p h c", c=64),
                         reads=[("bk", bank)], writes=[("stv", bank % 2)])
                k.dma("sp", Vd_[0:8, :, tb, :].rearrange("h p c -> p h c"), st[:, 0:8, :],
                      reads=[("stv", bank % 2)], writes=[("Vd", 0, tb)])
            b3 = load_slab(w_in, 1280, 512)
            proj_fm(b2, 0, 4, QTd_, [8 + i for i in range(8)])
            b4 = load_slab(w_in, 1792, 520)
            proj_fm(b3, 0, 4, KTd_, [8 + i for i in range(8)])
            b5 = load_slab(w_in, 2312, 512)
            proj_v(b4, 0, 8, 8)
            for (t0, W) in tgroups:
                bank = pa_rr[0] % 4
                pa_rr[0] += 1
                for kc in range(8):
                    k.op("pe", I("matmul",
                        ps[0:8, bank, 0:W], lhsT=wslab[:, b4, kc, 512:520], rhs=hT[:, kc, t0:t0 + W], start=(kc == 0), stop=(kc == 7)),
                        reads=[("slab", b4, kc)] + [("hT", t0 // 128 + j) for j in range(W // 128)], writes=[("bk", bank)])
                k.op("act", I("activation", out=fz_e[0:8, 0:W], in_=ps[0:8, bank, 0:W], func=AF.Exp, scale=-1.0,
                                                                   bias=small[0:8, 9:10]),
                     reads=[("bk", bank), "nbf"], writes=["fz_e"])
                k.op("act", I("activation", out=fz_sb[0:8, t0:t0 + W], in_=fz_e[0:8, 0:W], func=AF.Ln, bias=1.0),
                     reads=["fz_e"], writes=[("fz", t0)])
            prev = None
            for (t0, W) in tgroups:
                init = 0.0 if prev is None else fz_sb[0:8, t0 - 1:t0]
                k.op("dve", I("tensor_tensor_scan",
                    out=fz_sb[0:8, t0:t0 + W], data0=ones_f[0:8, 0:W], data1=fz_sb[0:8, t0:t0 + W], initial=init,
                    op0=ALU.mult, op1=ALU.add), reads=[("fz", t0), "ones_f"] + ([("fz", prev)] if prev is not None else []),
                    writes=[("fz", t0)])
                prev = t0
            allfz = [("fz", t0) for (t0, W) in tgroups]
            for tb in range(NB):
                k.op("pe", I("transpose", out=ps[:, 2, tb * 8:(tb + 1) * 8], in_=fz_sb[0:8, tb * 128:(tb + 1) * 128],
                                                        identity=ident_f[0:8, 0:8]),
                     reads=allfz + ["ident_f"], writes=[("bk", 2)])
            k.op("dve", I("tensor_copy", out=negcum[:].rearrange("p b h -> p (b h)"), in_=ps[:, 2, 0:NB * 8]),
                 reads=[("bk", 2)], writes=["negcum"])
            dg = fz_e[0:8, 0:8 * NG].rearrange("p (h g) -> p h g", g=NG)
            k.op("dve", I("memset", fz_e[0:8, 0:8 * NG], 0.0), reads=["fz_e"], writes=["fz_e"])
            for g in range(NG):
                lastq = min(NB, 4 * g + 4) * 128 - 1
                k.op("dve", I("tensor_tensor",
                    out=dg[:, :, g], in0=ident_f[0:8, 0:8], in1=fz_sb[0:8, lastq:lastq + 1].broadcast_to([8, 8]), op=ALU.mult),
                    reads=allfz + ["ident_f", "fz_e"], writes=["fz_e"])
            k.op("pe", I("matmul", ps[:, 3, 0:8 * NG], lhsT=ones_f[0:8, 0:128], rhs=fz_e[0:8, 0:8 * NG], start=True, stop=True),
                 reads=["fz_e", "ones_f"], writes=[("bk", 3)])
            k.op("dve", I("tensor_scalar_mul", out=crefB[:].rearrange("p h g -> p (h g)"), in0=ps[:, 3, 0:8 * NG], scalar1=-1.0),
                 reads=[("bk", 3)], writes=["crefB"])
            proj_gate(b5, 0, 0)
            b6 = load_slab(w_in, 2824, 512)
            proj_gate(b6, 0, 8)
        k.barrier()
        if stop == ("A", layer):
            k.emit()
            return nc

        wout_sb = wflat[:, 0:8 * 1024].rearrange("p (c n) -> p c n", n=1024)
        for kc in range(8):
            k.dma("pool", wout_sb[:, kc, :], w_out[kc * 128:(kc + 1) * 128, :], writes=[("wout", kc)])

        HB = NT + NT + NB * 65 + (NB % 2) + 2 * NB * 64
        assert 2 * HB <= 8 * NT

        def headbuf(s):
            o = s * HB
            qt = hT_flat[:, o:o + NT]
            kt = hT_flat[:, o + NT:o + 2 * NT]
            v = hT_flat[:, o + 2 * NT:o + 2 * NT + NB * 65].rearrange("p (b c) -> p b c", c=65)
            g = hT_flat[:, o + 2 * NT + NB * 65 + (NB % 2):o + 2 * NT + NB * 65 + (NB % 2) + 2 * NB * 64].bitcast(F32).rearrange("p (b c) -> p b c", c=64)
            return qt, kt, v, g

        hb_sets = [headbuf(0), headbuf(1)]
        _bb = Bump(work, WORK)
        e_f = [_bb.f32([128, 512]) for _ in range(2)]
        sp_f = [_bb.f32([128, 512]) for _ in range(2)]
        P_f = [_bb.f32([128, 514]) for _ in range(2)]
        arg_f = [_bb.f32([128, 512]) for _ in range(2)]
        wT_b = [_bb.bf16([128, 512]) for _ in range(2)]
        E_b = [_bb.bf16([128, 512]) for _ in range(2)]
        sadd = [_bb.f32([128, 3, 128]) for _ in range(2)]
        ssm2 = [_bb.f32([128, 256]) for _ in range(2)]
        Esm = [_bb.bf16([128, 128]) for _ in range(4)]
        swb_sbs = [_bb.f32([128, 5, 128]) for _ in range(2)]
        swsm_sbs = [_bb.f32([128, 2, 128]) for _ in range(2)]
        brow = [_bb.f32([128, 64]) for _ in range(2)]
        Cc = [small[:, 16 + i:17 + i] for i in range(4)]
        rden = [small[:, 24 + i:25 + i] for i in range(8)]
        k.op("dve", I("memset", P_f[0][:, 0:1], 0.0), writes=[("P", 0)])
        k.op("dve", I("memset", P_f[1][:, 0:1], 0.0), writes=[("P", 1)])

        for s_ in range(2):
            qt_, kt_, _, _ = hb_sets[s_]
            k.op("pool", I("memset", qt_[64:128, :], 0.0), writes=[("QT", s_)])
            k.op("pool", I("memset", kt_[64:128, :], 0.0), writes=[("KT", s_)])

        def load_head(h, s):
            qt, kt, v, g = hb_sets[s]
            rows = 96 if (even and h >= 8) else 64
            k.dma("sp", qt[0:rows, :], QTd_[h, 0:rows, :], writes=[("QT", s)])
            k.dma("sp", kt[0:rows, :], KTd_[h, 0:rows, :], writes=[("KT", s)])
            k.dma("sp", v, Vd_[h], writes=[("V", s)])
            k.dma("sp", g, Gd_[h], writes=[("G", s)])
            if (not even) and h < 8:
                k.dma("sp", swb_sbs[s], cd["swb"][h], writes=[("swb", s)])
                k.dma("sp", swsm_sbs[s][0:16], cd["swsm"][h], writes=[("swsm", s)])

        cnt = {"sb": 0, "rd": 0, "acc": 0}

        def sb_head(h, s):
            qt, kt, v, g = hb_sets[s]
            items = []
            for qb in range(NB):
                hi = (qb + 1) * 128
                nfull = hi // 512
                chunks = []
                if hi % 512:
                    chunks.append((nfull * 512, hi - nfull * 512))
                for c in range(nfull - 1, -1, -1):
                    chunks.append((c * 512, 512))
                for ci, (k0, W) in enumerate(chunks):
                    items.append(dict(qb=qb, k0=k0, W=W, first=(ci == 0), lastc=(ci == len(chunks) - 1), idx=len(items)))

            def zb(i):
                return i % 3

            def ab(i):
                return 3 + i % 2

            def accb(qb):
                return 5 + qb % 2

            def stA(i, it):
                qb, k0, W = it["qb"], it["k0"], it["W"]
                b = zb(i)
                k.op("pe", I("matmul", ps[:, b, 0:W], lhsT=qt[0:128, qb * 128:(qb + 1) * 128], rhs=kt[0:128, k0:k0 + W],
                                              start=True, stop=not it["first"]),
                     reads=[("QT", s), ("KT", s)], writes=[("z", b)])
                if it["first"]:
                    k.op("pe", I("matmul", ps[:, b, W - 128:W], lhsT=ident_b[:], rhs=mS[:], start=False, stop=True),
                         reads=["ident_b", "mS"], writes=[("z", b)])

            def stB(i, it):
                qb, k0, W = it["qb"], it["k0"], it["W"]
                b = zb(i)
                u = i % 2
                k.op("act", I("activation", out=e_f[u][:, 0:W], in_=ps[:, b, 0:W], func=AF.Exp), reads=[("z", b)], writes=[("e", u)])
                k.op("act", I("activation", out=sp_f[u][:, 0:W], in_=e_f[u][:, 0:W], func=AF.Ln, bias=1.0), reads=[("e", u)], writes=[("sp", u)])

            def stB2(i, it):
                qb, k0, W = it["qb"], it["k0"], it["W"]
                b = zb(i)
                u = i % 2
                pu = (i - 1) % 2
                if it["first"]:
                    init, rd = 0.0, []
                else:
                    init, rd = P_f[pu][:, 0:1], [("P", pu)]
                k.op("dve", I("tensor_tensor_scan", out=P_f[u][:, W - 1::-1] if W == 514 else P_f[u][:, 0:W][:, ::-1], data0=ones_f[:, 0:W],
                              data1=sp_f[u][:, 0:W][:, ::-1], initial=init, op0=ALU.mult, op1=ALU.add),
                     reads=[("sp", u), "ones_f"] + rd, writes=[("P", u)])
                k.op("dve", I("tensor_tensor", out=arg_f[u][:, 0:W], in0=ps[:, b, 0:W], in1=P_f[u][:, 0:W], op=ALU.subtract),
                     reads=[("z", b), ("P", u)], writes=[("arg", u)])
                a = ab(i)
                for j in range(W // 128):
                    k.op("pe", I("transpose", out=ps[:, a, j * 128:(j + 1) * 128], in_=arg_f[u][:, j * 128:(j + 1) * 128],
                                                          identity=ident_f[:]), reads=[("arg", u), "ident_f"], writes=[("aT", a)])

            def stC(i, it):
                qb, k0, W = it["qb"], it["k0"], it["W"]
                a = ab(i)
                u = i % 2
                k.op("act", I("activation", out=wT_b[u][:, 0:W], in_=ps[:, a, 0:W], func=AF.Exp), reads=[("aT", a)], writes=[("wT", u)])
                acc = accb(qb)
                nb = W // 128
                for j in range(nb):
                    kb = k0 // 128 + j
                    k.op("pe", I("matmul", ps[:, acc, 0:64], lhsT=wT_b[u][:, j * 128:(j + 1) * 128], rhs=v[:, kb, 0:64],
                                                              start=(it["first"] and j == 0), stop=(it["lastc"] and j == nb - 1)),
                         reads=[("wT", u), ("V", s)], writes=[("acc", acc)])

            def stD(i, it):
                qb = it["qb"]
                acc = accb(qb)
                if it["lastc"]:
                    k.op("dve", I("tensor_tensor", out=mixg[:, h // 2, qb, (h % 2) * 64:(h % 2) * 64 + 64], in0=ps[:, acc, 0:64], in1=g[:, qb, :], op=ALU.mult),
                         reads=[("acc", acc), ("G", s)], writes=[("mixg", h, qb)])

            pipeline(k, items, [stA, stB, stB2, stC, stD], [0, 1, 2, 3, 4])

        def kq_head(h, s, kind):
            qt, kt, v, g = hb_sets[s]
            KD = 96 if kind == "mla" else 128
            scale = (96.0 ** -0.5) if kind == "mla" else 0.125
            hl = h - 8
            items = []
            for gi in range(NG):
                q0b = 4 * gi
                nq = min(4, NB - q0b)
                kmax = min(q0b + nq, NB - 1) if kind == "mla" else q0b + nq - 1
                for kb in range(0, kmax + 1):
                    jlo = max(0, kb - q0b - (1 if kind == "mla" else 0))
                    items.append(dict(gi=gi, q0b=q0b, nq=nq, kb=kb, jlo=jlo, firstk=(kb == 0), lastk=(kb == kmax), kmax=kmax))

            def sbk(i):
                return i % 4

            def accb(gi):
                return 4 + gi % 2

            def stA(i, it):
                q0b, nq, kb, jlo = it["q0b"], it["nq"], it["kb"], it["jlo"]
                b = sbk(i)
                Wc = (nq - jlo) * 128
                c0 = (q0b + jlo) * 128
                masks = []
                for j in range(jlo, nq):
                    rel = (q0b + j) - kb
                    if rel == -1:
                        masks.append((j, mN, "mN"))
                    elif rel == 0:
                        masks.append((j, mM if kind == "mla" else mF, "mM" if kind == "mla" else "mF"))
                k.op("pe", I("matmul", ps[:, b, 0:Wc], lhsT=kt[0:KD, kb * 128:(kb + 1) * 128], rhs=qt[0:KD, c0:c0 + Wc],
                                              start=True, stop=(len(masks) == 0)),
                     reads=[("QT", s), ("KT", s)], writes=[("S", b)])
                for mi, (j, mt, mk) in enumerate(masks):
                    k.op("pe", I("matmul", ps[:, b, (j - jlo) * 128:(j - jlo + 1) * 128], lhsT=ident_b[:], rhs=mt[:],
                                                                    start=False, stop=(mi == len(masks) - 1)),
                         reads=["ident_b", mk], writes=[("S", b)])
                if kind == "fox" and it["firstk"]:
                    bi = it["gi"] % 2
                    k.op("dve", I("tensor_scalar", out=brow[bi][:, 0:NB], in0=negcum[:, :, hl], scalar1=crefB[:, hl, it["gi"]:it["gi"] + 1],
                                                          scalar2=None, op0=ALU.add),
                         reads=["negcum", "crefB"], writes=[("brow", bi)])

            def stB(i, it):
                q0b, nq, kb, jlo = it["q0b"], it["nq"], it["kb"], it["jlo"]
                b = sbk(i)
                u = i % 2
                Wc = (nq - jlo) * 128
                if kind == "fox":
                    bi = it["gi"] % 2
                    k.op("act", I("activation", out=E_b[u][:, 0:Wc], in_=ps[:, b, 0:Wc], func=AF.Exp, scale=scale, bias=brow[bi][:, kb:kb + 1]),
                         reads=[("S", b), ("brow", bi)], writes=[("E", u)])
                else:
                    k.op("act", I("activation", out=E_b[u][:, 0:Wc], in_=ps[:, b, 0:Wc], func=AF.Exp, scale=scale),
                         reads=[("S", b)], writes=[("E", u)])

            def stC(i, it):
                q0b, nq, kb, jlo, gi = it["q0b"], it["nq"], it["kb"], it["jlo"], it["gi"]
                u = i % 2
                acc = accb(gi)
                Wc = (nq - jlo) * 128
                k.op("pe", I("matmul", ps[0:65, acc, jlo * 128:nq * 128], lhsT=v[:, kb, :], rhs=E_b[u][:, 0:Wc],
                             start=it["firstk"], stop=it["lastk"], skip_group_check=True),
                     reads=[("E", u), ("V", s)], writes=[("acc", acc)])

            def stD(i, it):
                q0b, nq, gi = it["q0b"], it["nq"], it["gi"]
                if not it["lastk"]:
                    return
                acc = accb(gi)
                fb = 6 + gi % 2
                aS = e_f[gi % 2]
                W = nq * 128
                k.op("dve", I("tensor_copy", out=aS[0:65, 0:W], in_=ps[0:65, acc, 0:W]), reads=[("acc", acc)], writes=[("e", gi % 2)])
                for j in range(nq):
                    k.op("pe", I("transpose", out=ps[:, fb, j * 65:(j + 1) * 65], in_=aS[0:65, j * 128:(j + 1) * 128], identity=ident_f[0:65, 0:65]),
                         reads=[("e", gi % 2), "ident_f"], writes=[("fin", fb)])
                for j in range(nq):
                    qb = q0b + j
                    ri = cnt["rd"] % 8
                    cnt["rd"] += 1
                    k.op("dve", I("reciprocal", out=rden[ri], in_=ps[:, fb, j * 65 + 64:j * 65 + 65]),
                         reads=[("fin", fb)], writes=[("rden", ri)])
                    k.op("dve", I("scalar_tensor_tensor", out=mixg[:, h // 2, qb, (h % 2) * 64:(h % 2) * 64 + 64], in0=ps[:, fb, j * 65:j * 65 + 64],
                                  scalar=rden[ri], in1=g[:, qb, :], op0=ALU.mult, op1=ALU.mult),
                         reads=[("fin", fb), ("rden", ri), ("G", s)], writes=[("mixg", h, qb)])

            pipeline(k, items, [stA, stB, stC, stD], [0, 2, 3, 4])

        def swa_head(h, s):
            qt, kt, v, g = hb_sets[s]
            swb_sb, swsm_sb = swb_sbs[s], swsm_sbs[s]
            items = [dict(qb=qb) for qb in range(NB)]

            def stA(i, it):
                qb = it["qb"]
                b = i % 2
                qs = qt[0:128, qb * 128:(qb + 1) * 128]
                for ri, r in enumerate((-2, -1, 0)):
                    kb = qb + r
                    if kb < 0:
                        continue
                    k.op("pe", I("matmul", ps[:, b, ri * 128:(ri + 1) * 128], lhsT=kt[0:128, kb * 128:(kb + 1) * 128], rhs=qs,
                                                                start=True, stop=True, skip_group_check=True),
                         reads=[("QT", s), ("KT", s)], writes=[("S", b)])
                b2 = 2 + i % 2
                if qb + 1 < NB:
                    k.op("pe", I("matmul", ps[0:16, b2, 0:128], lhsT=kt[0:128, (qb + 1) * 128:(qb + 1) * 128 + 16], rhs=qs,
                                                  start=True, stop=True, skip_group_check=True),
                         reads=[("QT", s), ("KT", s)], writes=[("Sn", b2)])
                if qb >= 1:
                    k.op("pe", I("matmul", ps[0:16, b2, 128:256], lhsT=kt[0:128, 0:16], rhs=qs, start=True, stop=True, skip_group_check=True),
                         reads=[("QT", s), ("KT", s)], writes=[("Sm", b2)])

            def stB(i, it):
                qb = it["qb"]
                b = i % 2
                u = i % 2
                b2 = 2 + i % 2
                if qb >= 2:
                    k.op("dve", I("tensor_tensor", out=sadd[u][:].rearrange("p a b -> p (a b)"), in0=ps[:, b, 0:384],
                                  in1=swb_sb[:, 0:3, :].rearrange("p a b -> p (a b)"), op=ALU.add),
                         reads=[("S", b), ("swb", s)], writes=[("sadd", u, ri) for ri in range(3)])
                else:
                    for ri, r in enumerate((-2, -1, 0)):
                        kb = qb + r
                        if kb < 0:
                            continue
                        tile_i = ri
                        if qb == 0 and r == 0:
                            tile_i = 3
                        if qb == 1 and r == -1:
                            tile_i = 4
                        k.op("dve", I("tensor_tensor", out=sadd[u][:, ri, :], in0=ps[:, b, ri * 128:(ri + 1) * 128],
                                      in1=swb_sb[:, tile_i, :], op=ALU.add),
                             reads=[("S", b), ("swb", s)], writes=[("sadd", u, ri)])
                if qb + 1 < NB and qb >= 1:
                    k.op("dve", I("tensor_tensor", out=ssm2[u][0:16, :], in0=ps[0:16, b2, 0:256],
                                  in1=swsm_sb[0:16, :, :].rearrange("p a b -> p (a b)"), op=ALU.add),
                         reads=[("Sn", b2), ("Sm", b2), ("swsm", s)], writes=[("ssm", u), ("ssm", 2 + u)])
                elif qb + 1 < NB:
                    k.op("dve", I("tensor_tensor", out=ssm2[u][0:16, 0:128], in0=ps[0:16, b2, 0:128], in1=swsm_sb[0:16, 0, :], op=ALU.add),
                         reads=[("Sn", b2), ("swsm", s)], writes=[("ssm", u)])
                elif qb >= 1:
                    k.op("dve", I("tensor_tensor", out=ssm2[u][0:16, 128:256], in0=ps[0:16, b2, 128:256], in1=swsm_sb[0:16, 1, :], op=ALU.add),
                         reads=[("Sm", b2), ("swsm", s)], writes=[("ssm", 2 + u)])

            def stB2(i, it):
                qb = it["qb"]
                u = i % 2
                ris = [ri for ri, r in enumerate((-2, -1, 0)) if qb + r >= 0]
                lo, hi = ris[0], ris[-1] + 1
                k.op("act", I("activation", out=E_b[u][:, lo * 128:hi * 128], in_=sadd[u][:, lo:hi, :].rearrange("p a b -> p (a b)"),
                              func=AF.Exp, scale=0.125),
                     reads=[("sadd", u, ri) for ri in ris], writes=[("E", u, ri) for ri in ris])
                if qb + 1 < NB:
                    k.op("act", I("activation", out=Esm[u][0:16, :], in_=ssm2[u][0:16, 0:128], func=AF.Exp, scale=0.125),
                         reads=[("ssm", u)], writes=[("Esm", u)])
                if qb >= 1:
                    k.op("act", I("activation", out=Esm[2 + u][0:16, :], in_=ssm2[u][0:16, 128:256], func=AF.Exp, scale=0.125,
                                  bias=metab_sb[0:16, h, qb:qb + 1]),
                         reads=[("ssm", 2 + u), "metab"], writes=[("Esm", 2 + u)])

            def stC(i, it):
                qb = it["qb"]
                u = i % 2
                acc = 4 + i % 2
                mms = []
                for ri, r in enumerate((-2, -1, 0)):
                    kb = qb + r
                    if kb < 0:
                        continue
                    mms.append((E_b[u][:, ri * 128:(ri + 1) * 128], v[:, kb, :], [("E", u, ri)]))
                if qb + 1 < NB:
                    mms.append((Esm[u][0:16, :], v[0:16, qb + 1, :], [("Esm", u)]))
                if qb >= 1:
                    mms.append((Esm[2 + u][0:16, :], v[0:16, 0, :], [("Esm", 2 + u)]))
                for mi, (l_, r_, rk_) in enumerate(mms):
                    k.op("pe", I("matmul", ps[:, acc, 0:65], lhsT=l_, rhs=r_, start=(mi == 0), stop=(mi == len(mms) - 1)),
                         reads=rk_ + [("V", s)], writes=[("acc", acc)])

            def stD(i, it):
                qb = it["qb"]
                acc = 4 + i % 2
                ri_ = cnt["rd"] % 8
                cnt["rd"] += 1
                k.op("dve", I("tensor_tensor", out=rden[ri_], in0=ps[:, acc, 64:65], in1=esink[:, h:h + 1], op=ALU.add),
                     reads=[("acc", acc), "esink"], writes=[("rden", ri_)])
                k.op("dve", I("reciprocal", out=rden[ri_], in_=rden[ri_]), reads=[("rden", ri_)], writes=[("rden", ri_)])
                k.op("dve", I("scalar_tensor_tensor", out=mixg[:, h // 2, qb, (h % 2) * 64:(h % 2) * 64 + 64], in0=ps[:, acc, 0:64], scalar=rden[ri_], in1=g[:, qb, :],
                                                             op0=ALU.mult, op1=ALU.mult),
                     reads=[("acc", acc), ("rden", ri_), ("G", s)], writes=[("mixg", h, qb)])

            pipeline(k, items, [stA, stB, stB2, stC, stD], [0, 1, 2, 3, 4])

        load_head(0, 0)
        for h in range(16):
            s = h % 2
            if h + 1 < 16:
                load_head(h + 1, (h + 1) % 2)
            if even:
                if h < 8:
                    sb_head(h, s)
                else:
                    kq_head(h, s, "mla")
            else:
                if h < 8:
                    swa_head(h, s)
                else:
                    kq_head(h, s, "fox")
        k.barrier()
        if stop == ("B", layer):
            mixdbg = nc.dram_tensor("mixdbg", [128, 8, NB, 128], BF16, kind="ExternalOutput").ap()
            k.dma("sp", mixdbg, mixg, writes=["mixdbg"])
            k.barrier()
            k.emit()
            return nc

        _cb = Bump(work, WORK)
        hb3 = [_cb.f32([128, D]) for _ in range(5)]
        mT2 = [_cb.bf16([128, 8, 128]) for _ in range(2)]
        stats2 = [_cb.f32([128, 2, 6]) for _ in range(2)]
        gainb = _cb.f32([128, D])
        biasb = _cb.f32([128, D])
        k.dma("sp", gainb, ln_gain[layer:layer + 1, :].partition_broadcast(128), writes=["gainb"])
        k.dma("sp", biasb, ln_bias[layer:layer + 1, :].partition_broadcast(128), writes=["biasb"])
        def c0(i, tb):
            u3 = tb % 5
            k.dma("sp", hb3[u3], hbuf[tb * 128:(tb + 1) * 128, :], writes=[("hblk", u3)])
            pb0 = psb_bf(0)
            for c in range(8):
                k.op("pe", I("transpose", out=pb0[:, c * 128:(c + 1) * 128], in_=mixg[:, c, tb, :], identity=ident_b[:]),
                     reads=[("mixg", 2 * c, tb), ("mixg", 2 * c + 1, tb), "ident_b"], writes=[("pmT", 0)])
            evac(mT2[tb % 2], pb0.rearrange("p (c t) -> p c t", t=128), reads=[("pmT", 0)], writes=[("mT", tb % 2)])

        def c1(i, tb):
            yb = 1 + 2 * (tb % 2)
            for half in range(2):
                for c in range(8):
                    k.op("pe", I("matmul", ps[:, yb + half, :], lhsT=mT2[tb % 2][:, c, :], rhs=wout_sb[:, c, half * 512:(half + 1) * 512],
                                 start=(c == 0), stop=(c == 7)),
                         reads=[("mT", tb % 2), ("wout", c)], writes=[("py", yb + half)])

        def c2(i, tb):
            u3 = tb % 5
            yb = 1 + 2 * (tb % 2)
            z = hb3[u3]
            sm = small[:, 32 + 8 * (tb % 2):40 + 8 * (tb % 2)]
            st_ = stats2[tb % 2]
            for half in range(2):
                k.op("dve", I("scalar_tensor_tensor", out=z[:, half * 512:(half + 1) * 512], in0=z[:, half * 512:(half + 1) * 512],
                              scalar=DN_ALPHA, in1=ps[:, yb + half, :], op0=ALU.mult, op1=ALU.add),
                     reads=[("hblk", u3), ("py", yb + half)], writes=[("hblk", u3)])
                k.op("dve", I("bn_stats", out=st_[:, half, :], in_=z[:, half * 512:(half + 1) * 512]),
                     reads=[("hblk", u3)], writes=[("stats", tb % 2, half)])
            k.op("dve", I("bn_aggr", out=sm[:, 0:2], in_=st_.rearrange("p a b -> p (a b)")),
                 reads=[("stats", tb % 2, 0), ("stats", tb % 2, 1)], writes=[("mv", tb % 2)])
            k.op("act", I("activation", out=sm[:, 2:3], in_=sm[:, 1:2], func=AF.Ln, bias=LN_EPS), reads=[("mv", tb % 2)], writes=[("lnv", tb % 2)])
            k.op("act", I("activation", out=sm[:, 3:4], in_=sm[:, 2:3], func=AF.Exp, scale=-0.5), reads=[("lnv", tb % 2)], writes=[("rstd2", tb % 2)])

        def c3(i, tb):
            u3 = tb % 5
            z = hb3[u3]
            sm = small[:, 32 + 8 * (tb % 2):40 + 8 * (tb % 2)]
            k.op("dve", I("tensor_scalar", out=z, in0=z, scalar1=sm[:, 0:1], scalar2=sm[:, 3:4], op0=ALU.subtract, op1=ALU.mult),
                 reads=[("hblk", u3), ("mv", tb % 2), ("rstd2", tb % 2)], writes=[("hblk", u3)])
            k.op("pool", I("tensor_tensor", out=z, in0=z, in1=gainb, op=ALU.mult), reads=[("hblk", u3), "gainb"], writes=[("hblk", u3)])
            k.op("pool", I("tensor_tensor", out=z, in0=z, in1=biasb, op=ALU.add), reads=[("hblk", u3), "biasb"], writes=[("hblk", u3)])

        def c4(i, tb):
            u3 = tb % 5
            z = hb3[u3]
            if last:
                lo = max(tb * 128, 16)
                hi_ = min((tb + 1) * 128, 16 + S)
                if lo < hi_:
                    k.dma("sp", out[lo - 16:hi_ - 16, :], z[lo - tb * 128:hi_ - tb * 128, :], reads=[("hblk", u3)], writes=[("out", tb)])
            else:
                k.dma("sp", hbuf[tb * 128:(tb + 1) * 128, :], z, reads=[("hblk", u3)], writes=[("hbuf", tb)])
                for c in range(8):
                    k.op("pe", I("transpose", out=ps[:, 6 + c // 4, (c % 4) * 128:(c % 4 + 1) * 128],
                                 in_=z[:, c * 128:(c + 1) * 128], identity=ident_f[:]),
                         reads=[("hblk", u3), "ident_f"], writes=[("psT", c // 4)])
                evac(hT[:, :, tb * 128:(tb + 1) * 128], ps[:, 6:8, :].rearrange("p a (c t) -> p (a c) t", t=128),
                     reads=[("psT", 0), ("psT", 1)], writes=[("hT", tb)], eng="act")

        pipeline(k, list(range(NB)), [c0, c1, c2, c3, c4], [0, 1, 2, 3, 4])
        k.barrier()

    k.emit()
    return nc


_CACHE = {}


def run(inputs, NB, S, depth, n_cores):
    key = (NB, S, depth)
    if key not in _CACHE:
        _CACHE[key] = (build(NB, S, depth), make_consts(NB))
    nc, consts = _CACHE[key]
    n_even = (depth + 1) // 2
    n_odd = depth // 2
    shared = {}
    for name in ("meta_tokens", "g_cq", "g_ckv", "w_uq", "w_ukv", "b_forget", "sink_logits", "ln_gain", "ln_bias"):
        shared[name] = np.ascontiguousarray(np.asarray(inputs[name], dtype=np.float32))
    for name in ("w_in_even", "w_out_even"):
        shared[name] = np.ascontiguousarray(np.asarray(inputs[name], dtype=np.float32)[:n_even])
    for name in ("w_in_odd", "w_out_odd"):
        shared[name] = np.ascontiguousarray(np.asarray(inputs[name], dtype=np.float32)[:max(n_odd, 1)])
    for name in ("g_cq", "g_ckv", "w_uq", "w_ukv"):
        shared[name] = shared[name][:n_even]
    for name in ("b_forget", "sink_logits"):
        shared[name] = shared[name][:max(n_odd, 1)]
    shared["ln_gain"] = shared["ln_gain"][:depth]
    shared["ln_bias"] = shared["ln_bias"][:depth]
    for n, _ in CONST_SPECS:
        shared["c_" + n] = consts[n]
    x = np.asarray(inputs["x"], dtype=np.float32)
    in_maps = []
    for b in range(n_cores):
        m = dict(shared)
        m["x"] = np.ascontiguousarray(x[b])
        in_maps.append(m)
    res = run_bass_kernel_spmd(nc, in_maps, core_ids=list(range(n_cores)))
    return np.stack([np.asarray(r["out"]) for r in res.results], axis=0).astype(np.float32)


def kernel(**inputs):
    x = np.asarray(inputs["x"])
    B, S, _ = x.shape
    n = S + 16
    NB = -(-n // 128)
    return run(inputs, NB, S, 4, B)
```
